# Optimizing a Trainium2 kernel written in Bass

```python
import functools
import jax, jax.numpy as jnp
from jax import lax
import numpy as np

D_MODEL = 1024
BATCH = 4
SEQ = 8192
DEPTH = 4
DEC_BATCH = 32
DEC_SEQ = 16
PAST_LEN = 1024

CHUNK = 64
N_MIXERS = 2
N_FOX = (DEPTH + 1) // 2
N_POOL = DEPTH // 2
N_HEADS = 16
HEAD_DIM = D_MODEL // N_HEADS
ATTN_SCALE = HEAD_DIM ** -0.5
Q_BLOCK = 128
D_FF = 2816
PLE_DIM = 256
POOL_WINDOWS = (2, 4, 8, 16)
N_POOL_GROUPS = len(POOL_WINDOWS)
POOL_GROUP = D_MODEL // N_POOL_GROUPS
POOL_STATE = max(POOL_WINDOWS) - 1
ALPHA = (2.0 * DEPTH) ** 0.25
BETA = (8.0 * DEPTH) ** -0.25
LN_EPS = 1e-5

kernel_name = "fox_pool_macaron_deepnorm_stream_step"


def _layer_norm(x, g, b):
    xf = x.astype(jnp.float32)
    mu = jnp.mean(xf, axis=-1, keepdims=True)
    var = jnp.mean(jnp.square(xf - mu), axis=-1, keepdims=True)
    return ((xf - mu) * lax.rsqrt(var + LN_EPS) * g + b).astype(x.dtype)


def _post_norm(x, sub, g, b):
    return _layer_norm(ALPHA * x + sub, g, b)


def _swiglu(x, w_in, w_out):
    h = x @ w_in
    a, u = h[..., :D_FF], h[..., D_FF:]
    return (jax.nn.silu(a) * u) @ w_out


def _fox_project(x, w_in, b_f):
    B, T, _ = x.shape
    h = x @ w_in
    qkv = h[..., :3 * D_MODEL].reshape(B, T, 3, N_HEADS, HEAD_DIM)
    logf = jax.nn.log_sigmoid((h[..., 3 * D_MODEL:] + b_f).astype(jnp.float32))
    return qkv[:, :, 0], qkv[:, :, 1], qkv[:, :, 2], logf


def _fox_attend(q, f_q, pos_q, k, v, f_k, pos_k):
    s = jnp.einsum("bqhd,bkhd->bhqk", q, k).astype(jnp.float32) * ATTN_SCALE
    s = s + (f_q[..., :, None] - f_k[..., None, :])
    s = jnp.where(pos_k[None, :] <= pos_q[:, None], s, -jnp.inf)
    w = jax.nn.softmax(s, axis=-1).astype(v.dtype)
    return jnp.einsum("bhqk,bkhd->bqhd", w, v)


def _fox_prompt(x, w_in, b_f, w_o):
    B, S, _ = x.shape
    q, k, v, logf = _fox_project(x, w_in, b_f)
    F = jnp.cumsum(logf, axis=1).transpose(0, 2, 1)
    pos = jnp.arange(S)
    nb = S // Q_BLOCK
    q_blocks = q.reshape(B, nb, Q_BLOCK, N_HEADS, HEAD_DIM).transpose(1, 0, 2, 3, 4)
    f_blocks = F.reshape(B, N_HEADS, nb, Q_BLOCK).transpose(2, 0, 1, 3)
    p_blocks = pos.reshape(nb, Q_BLOCK)

    def one_block(args):
        qb, fb, pb = args
        return _fox_attend(qb, fb, pb, k, v, F, pos)

    o = lax.map(one_block, (q_blocks, f_blocks, p_blocks))
    o = o.transpose(1, 0, 2, 3, 4).reshape(B, S, D_MODEL)
    return o @ w_o, (k, v, logf)


def _fox_sample(x, cache_k, cache_v, cache_logf, w_in, b_f, w_o):
    B, T, _ = x.shape
    P = cache_k.shape[1]
    q, k, v, logf = _fox_project(x, w_in, b_f)
    logf_all = jnp.concatenate([cache_logf.astype(jnp.float32), logf], axis=1)
    F = jnp.cumsum(logf_all, axis=1).transpose(0, 2, 1)
    k_all = jnp.concatenate([cache_k, k.astype(cache_k.dtype)], axis=1)
    v_all = jnp.concatenate([cache_v, v.astype(cache_v.dtype)], axis=1)
    pos = jnp.arange(P + T)
    o = _fox_attend(q, F[:, :, P:], pos[P:], k_all, v_all, F, pos)
    return o.reshape(B, T, D_MODEL) @ w_o, (k, v, logf)


def _pool_mix(x_ext, n_hist, w_pool, scale):
    B, L, _ = x_ext.shape
    T = L - n_hist
    xf = x_ext.astype(jnp.float32)
    cs = jnp.concatenate([jnp.zeros((B, 1, D_MODEL), jnp.float32), jnp.cumsum(xf, axis=1)], axis=1)
    t = jnp.arange(n_hist, L)
    hi = cs[:, n_hist + 1:]
    groups = []
    for g, w in enumerate(POOL_WINDOWS):
        sl = slice(g * POOL_GROUP, (g + 1) * POOL_GROUP)
        lo_idx = jnp.maximum(t + 1 - w, 0)
        lo = jnp.take(cs[..., sl], lo_idx, axis=1)
        cnt = (t + 1 - lo_idx).astype(jnp.float32)
        groups.append((hi[..., sl] - lo) / cnt[None, :, None])
    pooled = jnp.concatenate(groups, axis=-1) - xf[:, n_hist:]
    pooled = pooled.reshape(B, T, N_POOL_GROUPS, POOL_GROUP).astype(x_ext.dtype)
    y = jnp.einsum("btgc,gcd->btgd", pooled, w_pool).reshape(B, T, D_MODEL)
    return y * scale


def _pool_prompt(x, w_pool, scale):
    return _pool_mix(x, 0, w_pool, scale), (x[:, -POOL_STATE:],)


def _pool_sample(x, state, w_pool, scale):
    ext = jnp.concatenate([state, x.astype(state.dtype)], axis=1)
    return _pool_mix(ext, POOL_STATE, w_pool, scale), (ext[:, -POOL_STATE:],)


def _layer(x, p_i, i, mixer, ln_g, ln_b, ffn_w_in, ffn_w_out, ple_w_proj, ple_w_gate, ple_b_gate):
    x = _post_norm(x, 0.5 * _swiglu(x, ffn_w_in[i, 0], ffn_w_out[i, 0]), ln_g[i, 0], ln_b[i, 0])
    m, state = mixer(x)
    x = _post_norm(x, m, ln_g[i, 1], ln_b[i, 1])
    x = _post_norm(x, 0.5 * _swiglu(x, ffn_w_in[i, 1], ffn_w_out[i, 1]), ln_g[i, 2], ln_b[i, 2])
    gate = jax.nn.sigmoid(x @ ple_w_gate[i] + ple_b_gate[i])
    x = _post_norm(x, (p_i @ ple_w_proj[i]) * gate, ln_g[i, 3], ln_b[i, 3])
    return x, state


def setup_inputs(seed: int = 0) -> dict:
    key = jax.random.key(seed)
    ks = jax.random.split(key, 24)
    f32 = jnp.float32
    nrm = lambda k, shape, s: jax.random.normal(k, shape, f32) * s
    b_f_base = jnp.linspace(1.0, 5.0, N_HEADS, dtype=f32)
    return {
        "x_prompt": nrm(ks[0], (BATCH, SEQ, D_MODEL), 1.0),
        "x_sample": nrm(ks[1], (DEC_BATCH, DEC_SEQ, D_MODEL), 1.0),
        "cache_fox_k": nrm(ks[2], (N_FOX, DEC_BATCH, PAST_LEN, N_HEADS, HEAD_DIM), 1.0),
        "cache_fox_v": nrm(ks[3], (N_FOX, DEC_BATCH, PAST_LEN, N_HEADS, HEAD_DIM), 1.0),
        "cache_fox_logf": jax.nn.log_sigmoid(3.0 + nrm(ks[4], (N_FOX, DEC_BATCH, PAST_LEN, N_HEADS), 1.0)),
        "state_pool": nrm(ks[5], (N_POOL, DEC_BATCH, POOL_STATE, D_MODEL), 1.0),
        "p_prompt": nrm(ks[6], (DEPTH, BATCH, SEQ, PLE_DIM), 1.0),
        "p_sample": nrm(ks[7], (DEPTH, DEC_BATCH, DEC_SEQ, PLE_DIM), 1.0),
        "ln_g": 1.0 + nrm(ks[8], (DEPTH, 4, D_MODEL), 0.02),
        "ln_b": nrm(ks[9], (DEPTH, 4, D_MODEL), 0.02),
        "ffn_w_in": nrm(ks[10], (DEPTH, 2, D_MODEL, 2 * D_FF), D_MODEL ** -0.5),
        "ffn_w_out": nrm(ks[11], (DEPTH, 2, D_FF, D_MODEL), BETA * D_FF ** -0.5),
        "fox_w_in": jnp.concatenate([
            nrm(ks[12], (N_FOX, D_MODEL, 3 * D_MODEL), D_MODEL ** -0.5),
            nrm(ks[13], (N_FOX, D_MODEL, N_HEADS), 0.5 * D_MODEL ** -0.5)], axis=-1),
        "fox_b_f": b_f_base[None, :] + nrm(ks[14], (N_FOX, N_HEADS), 0.1),
        "fox_w_o": nrm(ks[15], (N_FOX, D_MODEL, D_MODEL), BETA * D_MODEL ** -0.5),
        "pool_w": nrm(ks[16], (N_POOL, N_POOL_GROUPS, POOL_GROUP, POOL_GROUP), BETA * POOL_GROUP ** -0.5),
        "pool_scale": 1.0 + nrm(ks[17], (N_POOL, D_MODEL), 0.02),
        "ple_w_proj": nrm(ks[18], (DEPTH, PLE_DIM, D_MODEL), BETA * PLE_DIM ** -0.5),
        "ple_w_gate": nrm(ks[19], (DEPTH, D_MODEL, D_MODEL), D_MODEL ** -0.5),
        "ple_b_gate": nrm(ks[20], (DEPTH, D_MODEL), 0.02),
    }


def reference(x_prompt, x_sample, cache_fox_k, cache_fox_v, cache_fox_logf, state_pool,
              p_prompt, p_sample, ln_g, ln_b, ffn_w_in, ffn_w_out, fox_w_in, fox_b_f, fox_w_o,
              pool_w, pool_scale, ple_w_proj, ple_w_gate, ple_b_gate):
    shared = dict(ln_g=ln_g, ln_b=ln_b, ffn_w_in=ffn_w_in, ffn_w_out=ffn_w_out,
                  ple_w_proj=ple_w_proj, ple_w_gate=ple_w_gate, ple_b_gate=ple_b_gate)
    yp, ys = x_prompt, x_sample
    kp, vp, fp, poolp = [], [], [], []
    ksm, vsm, fsm, pools = [], [], [], []
    for i in range(DEPTH):
        j = i // N_MIXERS
        if i % N_MIXERS == 0:
            mix_p = functools.partial(_fox_prompt, w_in=fox_w_in[j], b_f=fox_b_f[j], w_o=fox_w_o[j])
            mix_s = functools.partial(_fox_sample, cache_k=cache_fox_k[j], cache_v=cache_fox_v[j],
                                      cache_logf=cache_fox_logf[j], w_in=fox_w_in[j],
                                      b_f=fox_b_f[j], w_o=fox_w_o[j])
            yp, (k1, v1, f1) = _layer(yp, p_prompt[i], i, mix_p, **shared)
            ys, (k2, v2, f2) = _layer(ys, p_sample[i], i, mix_s, **shared)
            kp.append(k1); vp.append(v1); fp.append(f1)
            ksm.append(k2); vsm.append(v2); fsm.append(f2)
        else:
            mix_p = functools.partial(_pool_prompt, w_pool=pool_w[j], scale=pool_scale[j])
            mix_s = functools.partial(_pool_sample, state=state_pool[j], w_pool=pool_w[j], scale=pool_scale[j])
            yp, (s1,) = _layer(yp, p_prompt[i], i, mix_p, **shared)
            ys, (s2,) = _layer(ys, p_sample[i], i, mix_s, **shared)
            poolp.append(s1); pools.append(s2)
    return (yp, ys, jnp.stack(kp), jnp.stack(vp), jnp.stack(fp), jnp.stack(poolp),
            jnp.stack(ksm), jnp.stack(vsm), jnp.stack(fsm), jnp.stack(pools))
```

```python
import numpy as np
from contextlib import ExitStack
import concourse.bass as bass
import concourse.mybir as mybir
from concourse.bass_utils import run_bass_kernel_spmd

F32 = mybir.dt.float32
BF16 = mybir.dt.bfloat16
AF = mybir.ActivationFunctionType
ALU = mybir.AluOpType

D = 1024
NH = 16
DH = 64
DFF = 2816
PLE = 256
DEPTH = 4
TT = 16
ALPHA = (2.0 * DEPTH) ** 0.25
LN_EPS = 1e-5
POOL_W = (2, 2, 4, 4, 8, 8, 16, 16)
NEG = -1.0e30

ENGS = ("pe", "act", "dve", "pool", "sp")


class _Op:
    __slots__ = ("eng", "fn", "deps", "is_dma", "dkey", "idx", "has_dep", "inc_idx", "dma_val")

    def __init__(self, eng, fn, is_dma, dkey):
        self.eng = eng
        self.fn = fn
        self.deps = []
        self.is_dma = is_dma
        self.dkey = dkey
        self.has_dep = False
        self.inc_idx = None
        self.dma_val = None


class Sched:
    def __init__(self):
        self.ops = []
        self.last_w = {}
        self.readers = {}
        self.dma_last = {}
        self.dma_cnt = {}

    def _add(self, op, reads, writes):
        deps = set()
        for k in reads:
            w = self.last_w.get(k)
            if w is not None:
                deps.add(w)
        for k in writes:
            w = self.last_w.get(k)
            if w is not None:
                deps.add(w)
            for r in self.readers.get(k, ()):
                deps.add(r)
        deps.discard(op)
        op.deps = list(deps)
        for d in op.deps:
            d.has_dep = True
        for k in reads:
            self.readers.setdefault(k, []).append(op)
        for k in writes:
            self.last_w[k] = op
            self.readers[k] = []
        op.idx = len(self.ops)
        self.ops.append(op)
        return op

    def op(self, eng, fn, reads=(), writes=()):
        return self._add(_Op(eng, fn, False, None), list(reads), list(writes))

    def dma(self, eng, fn, dkey, reads=(), writes=()):
        op = _Op(eng, fn, True, dkey)
        prev = self.dma_last.get(dkey)
        self._add(op, list(reads), list(writes))
        if prev is not None and prev not in op.deps:
            op.deps.append(prev)
            prev.has_dep = True
        self.dma_last[dkey] = op
        self.dma_cnt[dkey] = self.dma_cnt.get(dkey, 0) + 1
        op.dma_val = 16 * self.dma_cnt[dkey]
        return op

    def emit(self, nc, final_wait_eng="sp"):
        dkeys = list(self.dma_cnt.keys())
        cnt = {e: 0 for e in ENGS}
        for o in self.ops:
            if not o.is_dma and o.has_dep:
                cnt[o.eng] += 1
                o.inc_idx = cnt[o.eng]
        with ExitStack() as es:
            esem = {e: es.enter_context(nc.semaphore("s_" + e)) for e in ENGS}
            dsem = {k: es.enter_context(nc.semaphore("d_%d" % i)) for i, k in enumerate(dkeys)}
            block = es.enter_context(nc.Block())
            per_eng = {e: [o for o in self.ops if o.eng == e] for e in ENGS}

            def run_stream(e, engobj):
                known = {}
                for o in per_eng[e]:
                    need = {}
                    for d in o.deps:
                        if d.is_dma:
                            s, v = dsem[d.dkey], d.dma_val
                        else:
                            if d.eng == "pe" and e == "pe" and not o.is_dma:
                                continue
                            s, v = esem[d.eng], d.inc_idx
                        if need.get(s, 0) < v:
                            need[s] = v
                    for s, v in need.items():
                        if known.get(s, 0) < v:
                            engobj.wait_ge(s, v)
                            known[s] = v
                    ins = o.fn(engobj)
                    if o.is_dma:
                        ins.then_inc(dsem[o.dkey], 16)
                    elif o.inc_idx is not None:
                        ins.then_inc(esem[e], 1)
                if e == final_wait_eng:
                    for k in dkeys:
                        engobj.wait_ge(dsem[k], 16 * self.dma_cnt[k])
                    for e2 in ENGS:
                        if cnt[e2] > 0:
                            engobj.wait_ge(esem[e2], cnt[e2])

            @block.tensor
            def _(eng):
                run_stream("pe", eng)

            @block.scalar
            def _(eng):
                run_stream("act", eng)

            @block.vector
            def _(eng):
                run_stream("dve", eng)

            @block.gpsimd
            def _(eng):
                run_stream("pool", eng)

            @block.sync
            def _(eng):
                run_stream("sp", eng)


def KS(name, c0, c1, g=512):
    return [(name, b) for b in range(c0 // g, (c1 - 1) // g + 1)]


def build_program(S, NBS, P, depth=DEPTH):
    NS = NBS * TT
    SC = S + NS
    NKB = S // 128
    NTL = S // 512
    NFOX = (depth + 1) // 2
    NPOOL = depth // 2
    assert NS <= 128 and S % 512 == 0 and P % 128 == 0 and P <= 1024
    nc = bass.Bass("TRN2", target_bir_lowering=False)

    def din(name, shape, dt=F32):
        return nc.dram_tensor(name, list(shape), dt, kind="ExternalInput").ap()

    def dout(name, shape, dt=F32):
        return nc.dram_tensor(name, list(shape), dt, kind="ExternalOutput").ap()

    def dscr(name, shape, dt):
        return nc.dram_tensor(name, list(shape), dt, kind="Internal").ap()

    xT = din("xT", [D, S]); pT = din("pT", [depth, PLE, S])
    xsT = din("xsT", [D, NS]); psT = din("psT", [depth, PLE, NS])
    ckT = din("ckT", [NFOX, NBS, NH, DH, P]); cv = din("cv", [NFOX, NBS, P, D])
    clf = din("clf", [NFOX, NBS * NH, P]); spool = din("spool", [NPOOL, D, NBS, 15])
    w_in = din("ffn_w_in", [depth, 2, D, 2 * DFF]); w_out = din("ffn_w_out", [depth, 2, DFF, D])
    fw_in = din("fox_w_in", [NFOX, D, 3 * D + NH]); fw_o = din("fox_w_o", [NFOX, D, D])
    pw = din("pool_w", [NPOOL, 4, 256, 256])
    wproj = din("ple_w_proj", [depth, PLE, D]); wgate = din("ple_w_gate", [depth, D, D])
    lng_d = din("lng", [128, depth * 32]); lnb_d = din("lnb", [128, depth * 32])
    bg_d = din("bgate", [128, depth * 8]); psc_d = din("pscale", [128, NPOOL * 8])
    bf_d = din("bf", [16, NFOX])
    maskp_d = din("maskp", [128, 2048]); masks_d = din("masks", [128, NBS * TT])
    invc_d = din("invc", [128, 128])

    yT = dout("yT", [D, S]); ysT = dout("ysT", [D, NS])
    kT_o = dout("kT_o", [NFOX, NH, DH, S]); v_o = dout("v_o", [NFOX, S, D]); lf_o = dout("lf_o", [NFOX, NH, S])
    pool_o = dout("pool_o", [NPOOL, D, 15])
    kTs_o = dout("kTs_o", [NFOX, NH, DH, NS]); vs_o = dout("vs_o", [NFOX, NS, D]); lfs_o = dout("lfs_o", [NFOX, NH, NS])
    pools_o = dout("pools_o", [NPOOL, D, NBS, 15])

    qTs = dscr("qTs", [NH, 70, SC], BF16); kTs = dscr("kTs", [NH, 70, SC], BF16)
    vBs = dscr("vBs", [NH, 128, NKB, 65], BF16); oTs = dscr("oTs", [NH, DH, SC], BF16)
    x1s = dscr("x1s", [D, SC], F32); ssS = dscr("ssS", [NBS * NH, 3, P], BF16)
    wsc_in = dscr("wsc_in", [depth, 2, 11, 128, 4096], BF16)
    wsc_out = dscr("wsc_out", [depth, 2, 4, 128, 5632], BF16)
    wsc_fq = dscr("wsc_fq", [NFOX, 6, 128, 4096], BF16)
    wsc_ff = dscr("wsc_ff", [NFOX, 128, 128], BF16)
    wsc_fo = dscr("wsc_fo", [NFOX, 4, 128, 2048], BF16)
    wsc_pw = dscr("wsc_pw", [NPOOL, 128, 2048], BF16)
    wsc_g = dscr("wsc_g", [depth, 2, 128, 4096], BF16)
    wsc_p = dscr("wsc_p", [depth, 2, 128, 1024], BF16)

    S_ = Sched()
    es = ExitStack()
    with es:
        def sb(name, shape, dt):
            return es.enter_context(nc.sbuf_tensor(name, list(shape), dt))

        X = sb("X", [128, 4096], F32)
        XB = sb("XB", [128, 4224], BF16)
        B1 = sb("B1", [128, 16384], BF16)
        Z = sb("Z", [128, 4096], F32)
        B2 = sb("B2", [128, 8192], BF16)
        MEANS = sb("MEANS", [128, 512], F32); VAR = sb("VAR", [128, 512], F32); RSTD = sb("RSTD", [128, 512], F32)
        T1 = [sb("T1_%d" % i, [128, 512], F32) for i in range(2)]
        T2 = [sb("T2_%d" % i, [128, 512], F32) for i in range(2)]
        SG = [sb("SG_%d" % i, [128, 512], F32) for i in range(2)]
        NWIN = 3
        WIN = [sb("WIN_%d" % i, [128, 4096], BF16) for i in range(NWIN)]
        WOT = sb("WOT", [128, 2 * 5632], BF16)
        VF = [sb("VF_%d" % i, [128, 512], F32) for i in range(2)]
        VBt = [sb("VBt_%d" % i, [128, 1040], BF16) for i in range(2)]
        VNS = sb("VNS", [128, 1024], BF16)
        PBt = sb("PBt", [128, 1024], BF16)
        PTT = sb("PTT", [128, 1536], BF16)
        OTT = sb("OTT", [128, 1536], BF16)
        ONES3 = sb("ONES3", [16, 1536], BF16)
        ZERB = sb("ZERB", [128, 1024], F32)
        FLAST = sb("FLAST", [16, 1], F32)
        W3 = [sb("W3_%d" % i, [128, 528], F32) for i in range(3)]
        HIST = sb("HIST", [128, 128], F32)
        SPOOL = sb("SPOOL", [128, 8 * NBS * 15], F32)
        RR = sb("RR", [65, 512], F32); BCs = sb("BCs", [64, 512], F32)
        RRs = sb("RRs", [128, 16], F32)
        OSALL = sb("OSALL", [128, 1024], BF16)
        MASK = sb("MASK", [128, 2048], BF16); MASKS = sb("MASKS", [128, NBS * TT], F32)
        INVC = sb("INVC", [128, 128], F32)
        ONESB = sb("ONESB", [128, 128], BF16); ONES128 = sb("ONES128", [128, 128], BF16)
        ONESF = sb("ONESF", [65, 128], F32)
        LNG = sb("LNG", [128, depth * 32], F32); LNB = sb("LNB", [128, depth * 32], F32)
        BG = sb("BG", [128, depth * 8], F32); PSC = sb("PSC", [128, NPOOL * 8], F32)
        NBF = sb("NBF", [16, NFOX], F32); EPSV = sb("EPSV", [128, 1], F32); ONEV = sb("ONEV", [128, 1], F32)
        PSG = [es.enter_context(nc.psum_tensor("pg%d" % i, [128, 1024], F32)) for i in range(3)]
        PS67 = [es.enter_context(nc.psum_tensor("pb%d" % i, [128, 512], F32)) for i in (6, 7)]
        PS = [PSG[i // 2][:, (i % 2) * 512:(i % 2) * 512 + 512] for i in range(6)] + [t[:, :] for t in PS67]
        PTT2 = sb("PTT2", [128, 1024], BF16)
        PTT3 = sb("PTT3", [128, 1024], BF16)
        PK = [("ps", i) for i in range(8)]

        def v3(ap, inner):
            return ap.rearrange("p (a b) -> p a b", b=inner)

        AQ = "pool"

        def ld(eng, out, in_, key):
            S_.dma(eng, lambda e: e.dma_start(out=out, in_=in_), key, writes=[key])

        ld("sp", LNG[:, :], lng_d, "LNG"); ld("sp", LNB[:, :], lnb_d, "LNB")
        ld("sp", BG[:, :], bg_d, "BG"); ld("sp", PSC[:, :], psc_d, "PSC")
        ld("sp", NBF[:, :], bf_d, "NBF"); ld("sp", MASKS[:, :], masks_d, "MASKS")
        ld("sp", INVC[:, :], invc_d, "INVC"); ld("pool", MASK[:, :], maskp_d, "MASK")
        S_.op("dve", lambda e: e.tensor_scalar(out=NBF[:, :], in0=NBF[:, :], scalar1=-1.0, scalar2=None, op0=ALU.mult),
              reads=["NBF"], writes=["NBF"])
        for t, val, key in ((ONES3, 1.0, "ONES3"), (ZERB, 0.0, "ZERB"), (ONESB, 1.0 / 1024.0, "ONESB"),
                            (ONES128, 1.0, "ONES128"), (ONESF, 1.0, "ONESF"), (EPSV, LN_EPS, "EPSV"),
                            (ONEV, 1.0, "ONEV"), (RR, 1.0, "RR"), (HIST, 0.0, "HIST")):
            S_.op("dve", lambda e, t=t, val=val: e.memset(t[:, :], val), writes=[key])
        for i in range(2):
            S_.op("dve", lambda e, i=i: e.memset(VBt[i][:, :], 1.0), writes=[("VBt", i)])
        S_.op("dve", lambda e: e.memset(VNS[:, :], 1.0), writes=["VNS"])
        for i in range(3):
            S_.op("dve", lambda e, i=i: e.memset(W3[i][:, :], 0.0), writes=[("W3", i)])

        ring = {"win": 0, "wo": 0}

        ncast = [0]

        def cast(dst, src, key):
            k = ("cast", ncast[0] % 8)
            ncast[0] += 1
            S_.dma("pool", lambda e: e.dma_start(out=dst, in_=src), k, writes=[key])

        def cast_ffn(l, k):
            wi = w_in[l, k].rearrange("(c p) f -> p c f", p=128)
            wo = w_out[l, k].rearrange("(j p) d -> p j d", p=128)
            for s in range(11):
                cast(v3(wsc_in[l, k, s], 512)[:, :, 0:256], wi[:, :, s * 256:(s + 1) * 256], ("wsc_in", l, k, s, 0))
                cast(v3(wsc_in[l, k, s], 512)[:, :, 256:512], wi[:, :, DFF + s * 256:DFF + (s + 1) * 256], ("wsc_in", l, k, s, 1))
            for s2 in range(4):
                cast(v3(wsc_out[l, k, s2], 256), wo[:, :, s2 * 256:(s2 + 1) * 256], ("wsc_out", l, k, s2))

        def cast_fox_in(jf):
            fw = fw_in[jf].rearrange("(c p) f -> p c f", p=128)
            for i in range(6):
                cast(v3(wsc_fq[jf, i], 512), fw[:, :, i * 512:(i + 1) * 512], ("wsc_fq", jf, i))
            cast(v3(wsc_ff[jf], 16), fw[:, :, 3072:3088], ("wsc_ff", jf))

        def cast_fox_o(jf):
            wov = fw_o[jf].rearrange("(c p) d -> p c d", p=128)
            for s2 in range(4):
                cast(v3(wsc_fo[jf, s2], 256), wov[:, :, s2 * 256:(s2 + 1) * 256], ("wsc_fo", jf, s2))

        def cast_ple(l):
            wg = wgate[l].rearrange("(c p) d -> p c d", p=128)
            wp = wproj[l].rearrange("(k p) d -> p k d", p=128)
            for s in range(2):
                cast(v3(wsc_g[l, s], 512), wg[:, :, s * 512:(s + 1) * 512], ("wsc_g", l, s))
                cast(v3(wsc_p[l, s], 512), wp[:, :, s * 512:(s + 1) * 512], ("wsc_p", l, s))

        def cast_pool(jp):
            cast(v3(wsc_pw[jp], 256), pw[jp].rearrange("g (cc p) d -> p (g cc) d", p=128), ("wsc_pw", jp))

        def win_ld(dst_of_slot, src, keys):
            slot = ring["win"] % NWIN
            ring["win"] += 1
            dst = dst_of_slot(WIN[slot])
            S_.dma("sp", lambda e: e.dma_start(out=dst, in_=src), ("win", slot), reads=keys, writes=[("win", slot)])
            return slot

        def wo_ld(ncols, src, keys):
            slot = ring["wo"] % 2
            ring["wo"] += 1
            dst = WOT[:, slot * 5632:slot * 5632 + ncols]
            S_.dma("sp", lambda e: e.dma_start(out=dst, in_=src), ("wo", slot), reads=keys, writes=[("wo", slot)])
            return slot

        def mm(out, lhsT, rhs, start, stop, reads, writes):
            S_.op("pe", lambda e: e.matmul(out, lhsT=lhsT, rhs=rhs, start=start, stop=stop), reads=reads, writes=writes)

        def act(out, in_, func, reads, writes, bias=None, scale=1.0):
            if bias is None:
                S_.op("act", lambda e: e.activation(out=out, in_=in_, func=func, scale=scale), reads=reads, writes=writes)
            else:
                S_.op("act", lambda e: e.activation(out=out, in_=in_, func=func, bias=bias, scale=scale),
                      reads=reads, writes=writes)

        def dve(fn, reads, writes):
            S_.op("dve", fn, reads=reads, writes=writes)

        def layer_norm(gi, N):
            psm, psq = PS[6], PS[7]
            for c in range(8):
                zc = Z[:, c * 512:c * 512 + N]
                act(B2[:, c * 512:c * 512 + N], zc, AF.Copy, [("Z", c)], [("B2", c)])
                act(B2[:, 4096 + c * 512:4096 + c * 512 + N], zc, AF.Square, [("Z", c)], [("B2", 8 + c)])
                mm(psm[:, :N], ONESB[:, :], B2[:, c * 512:c * 512 + N], c == 0, c == 7, [("B2", c), "ONESB"], [PK[6]])
                mm(psq[:, :N], ONESB[:, :], B2[:, 4096 + c * 512:4096 + c * 512 + N], c == 0, c == 7,
                   [("B2", 8 + c), "ONESB"], [PK[7]])
            act(MEANS[:, :N], psm[:, :N], AF.Copy, [PK[6]], ["MEANS"])
            act(RSTD[:, :N], psm[:, :N], AF.Square, [PK[6]], ["RSTD"])
            dve(lambda e: e.tensor_tensor(out=VAR[:, :N], in0=psq[:, :N], in1=RSTD[:, :N], op=ALU.subtract),
                [PK[7], "RSTD"], ["VAR"])
            act(VAR[:, :N], VAR[:, :N], AF.Sqrt, ["VAR", "EPSV"], ["VAR"], bias=EPSV[:, 0:1])
            dve(lambda e: e.reciprocal(out=RSTD[:, :N], in_=VAR[:, :N]), ["VAR"], ["RSTD"])
            for c in range(8):
                t1, t2 = T1[c % 2], T2[c % 2]
                zc = Z[:, c * 512:c * 512 + N]
                g = LNG[:, gi * 8 + c:gi * 8 + c + 1]
                b = LNB[:, gi * 8 + c:gi * 8 + c + 1]
                dve(lambda e, t1=t1, zc=zc: e.tensor_tensor(out=t1[:, :N], in0=zc, in1=MEANS[:, :N], op=ALU.subtract),
                    [("Z", c), "MEANS"], [("T1", c % 2)])
                dve(lambda e, t1=t1, t2=t2, g=g: e.scalar_tensor_tensor(out=t2[:, :N], in0=t1[:, :N], scalar=g,
                                                                        in1=RSTD[:, :N], op0=ALU.mult, op1=ALU.mult),
                    [("T1", c % 2), "RSTD", "LNG"], [("T2", c % 2)])
                act(XB[:, c * 512:c * 512 + N], t2[:, :N], AF.Identity, [("T2", c % 2), "LNB"], [("XB", c)], bias=b)
                act(X[:, c * 512:c * 512 + N], t2[:, :N], AF.Identity, [("T2", c % 2), "LNB"], [("X", c)], bias=b)

        def ffn(l, k, N):
            for s in range(11):
                slot = win_ld(lambda w: w[:, 0:4096], wsc_in[l, k, s], [("wsc_in", l, k, s, 0), ("wsc_in", l, k, s, 1)])
                for jj in range(2):
                    j = 2 * s + jj
                    pa, pu = PS[j % 2], PS[2 + j % 2]
                    for c in range(8):
                        mm(pa[:, :N], WIN[slot][:, c * 512 + jj * 128:c * 512 + jj * 128 + 128], XB[:, c * 512:c * 512 + N],
                           c == 0, c == 7, [("win", slot), ("XB", c)], [PK[j % 2]])
                    for c in range(8):
                        mm(pu[:, :N], WIN[slot][:, c * 512 + 256 + jj * 128:c * 512 + 256 + jj * 128 + 128],
                           XB[:, c * 512:c * 512 + N], c == 0, c == 7, [("win", slot), ("XB", c)], [PK[2 + j % 2]])
                    sg = SG[j % 2]
                    act(sg[:, :N], pa[:, :N], AF.Silu, [PK[j % 2]], [("SG", j % 2)])
                    dve(lambda e, sg=sg, pu=pu, j=j: e.scalar_tensor_tensor(
                        out=B1[:, j * 512:j * 512 + N], in0=sg[:, :N], scalar=0.5, in1=pu[:, :N], op0=ALU.mult, op1=ALU.mult),
                        [("SG", j % 2), PK[2 + j % 2]], [("B1", j)])
            for s2 in range(4):
                slot = wo_ld(5632, wsc_out[l, k, s2], [("wsc_out", l, k, s2)])
                for cc in range(2):
                    c = 2 * s2 + cc
                    py = PS[4 + c % 2]
                    for j in range(22):
                        o0 = slot * 5632 + j * 256 + cc * 128
                        mm(py[:, :N], WOT[:, o0:o0 + 128], B1[:, j * 512:j * 512 + N], j == 0, j == 21,
                           [("wo", slot), ("B1", j)], [PK[4 + c % 2]])
                    dve(lambda e, c=c, py=py: e.scalar_tensor_tensor(
                        out=Z[:, c * 512:c * 512 + N], in0=X[:, c * 512:c * 512 + N], scalar=ALPHA, in1=py[:, :N],
                        op0=ALU.mult, op1=ALU.add), [("X", c), PK[4 + c % 2]], [("Z", c)])
            layer_norm(l * 4 + 2 * k, N)

        def ple(l, tl):
            N, c0 = tl["N"], tl["c0"]
            psrc = (psT if tl["smp"] else pT)[l].rearrange("(k p) s -> p k s", p=128)
            pcol = 0 if tl["smp"] else c0
            S_.dma("pool", lambda e: e.dma_start(out=v3(PBt[:, :], 512)[:, :, 0:N], in_=psrc[:, :, pcol:pcol + N]),
                   "PBt", writes=["PBt"])
            for s in range(2):
                sg_ = win_ld(lambda w: w[:, 0:4096], wsc_g[l, s], [("wsc_g", l, s)])
                sp_ = win_ld(lambda w: w[:, 0:1024], wsc_p[l, s], [("wsc_p", l, s)])
                for cc in range(4):
                    c = 4 * s + cc
                    pg, pp = PS[c % 2], PS[2 + c % 2]
                    for kc in range(8):
                        mm(pg[:, :N], WIN[sg_][:, kc * 512 + cc * 128:kc * 512 + cc * 128 + 128], XB[:, kc * 512:kc * 512 + N],
                           kc == 0, kc == 7, [("win", sg_), ("XB", kc)], [PK[c % 2]])
                    for kk in range(2):
                        mm(pp[:, :N], WIN[sp_][:, kk * 512 + cc * 128:kk * 512 + cc * 128 + 128], PBt[:, kk * 512:kk * 512 + N],
                           kk == 0, kk == 1, [("win", sp_), "PBt"], [PK[2 + c % 2]])
                    sg = SG[c % 2]
                    act(sg[:, :N], pg[:, :N], AF.Sigmoid, [PK[c % 2], "BG"], [("SG", c % 2)], bias=BG[:, l * 8 + c:l * 8 + c + 1])
                    t1 = T1[c % 2]
                    dve(lambda e, sg=sg, pp=pp, t1=t1: e.tensor_tensor(out=t1[:, :N], in0=sg[:, :N], in1=pp[:, :N], op=ALU.mult),
                        [("SG", c % 2), PK[2 + c % 2]], [("T1", c % 2)])
                    dve(lambda e, c=c, t1=t1: e.scalar_tensor_tensor(
                        out=Z[:, c * 512:c * 512 + N], in0=X[:, c * 512:c * 512 + N], scalar=ALPHA, in1=t1[:, :N],
                        op0=ALU.mult, op1=ALU.add), [("X", c), ("T1", c % 2)], [("Z", c)])
            layer_norm(l * 4 + 3, N)

        def foxproj(jf, tl):
            N, c0, smp = tl["N"], tl["c0"], tl["smp"]
            fw = fw_in[jf].rearrange("(c p) f -> p c f", p=128)
            qv = qTs.rearrange("h p s -> p h s")
            kv = kTs.rearrange("h p s -> p h s")
            kov = (kTs_o if smp else kT_o)[jf].rearrange("h p s -> p h s")
            oc0 = 0 if smp else c0
            import os
            ksub = int(os.environ.get("KSUB", "9"))
            if ksub <= 0:
                return
            qv2 = qTs.rearrange("(hp two) p s -> two p hp s", two=2)
            kv2 = kTs.rearrange("(hp two) p s -> two p hp s", two=2)
            kov2 = (kTs_o if smp else kT_o)[jf].rearrange("(hp two) p s -> two p hp s", two=2)
            for which in range(2):
                for sq in range(2):
                    slot = win_ld(lambda w: w[:, 0:4096], wsc_fq[jf, which * 2 + sq], [("wsc_fq", jf, which * 2 + sq)])
                    stg0 = which * 4096
                    for pr in range(4):
                        pq = PS[pr % 2]
                        for c in range(8):
                            mm(pq[:, :N], WIN[slot][:, c * 512 + pr * 128:c * 512 + pr * 128 + 128], XB[:, c * 512:c * 512 + N],
                               c == 0, c == 7, [("win", slot), ("XB", c)], [PK[pr % 2]])
                        if which == 0:
                            act(B2[:, stg0 + pr * 512:stg0 + pr * 512 + N], pq[:, :N], AF.Copy, [PK[pr % 2]],
                                [("B2", which * 8 + pr)], scale=0.125)
                        else:
                            act(Z[:, pr * 512:pr * 512 + N], pq[:, :N], AF.Copy, [PK[pr % 2]], [("Z", pr)])
                            dve(lambda e, pr=pr: e.tensor_copy(out=B2[:, stg0 + pr * 512:stg0 + pr * 512 + N],
                                                               in_=Z[:, pr * 512:pr * 512 + N]),
                                [("Z", pr)], [("B2", which * 8 + pr)])
                    rk = [("B2", which * 8 + i) for i in range(4)]
                    for two in range(2):
                        stg = v3(B2[two * 64:two * 64 + 64, stg0:stg0 + 2048], 512)[:, :, 0:N]
                        dstv = (qv2 if which == 0 else kv2)[two, 0:64, sq * 4:(sq + 1) * 4, c0:c0 + N]
                        S_.dma(AQ, lambda e, dstv=dstv, stg=stg: e.dma_start(out=dstv, in_=stg), ("B2st", which), reads=rk,
                               writes=[("qk_scr", which, c0)])
                        if which == 1:
                            kf = v3(Z[two * 64:two * 64 + 64, 0:2048], 512)[:, :, 0:N]
                            dk = kov2[two, :, sq * 4:(sq + 1) * 4, oc0:oc0 + N]
                            S_.dma(AQ, lambda e, dk=dk, kf=kf: e.dma_start(out=dk, in_=kf), "Zst",
                                   reads=[("Z", i) for i in range(4)])
            if ksub <= 1:
                return
            sv = [win_ld(lambda w: w[:, 0:4096], wsc_fq[jf, 4 + i], [("wsc_fq", jf, 4 + i)]) for i in range(2)]
            vov = (vs_o if smp else v_o)[jf]
            for tb in range(N // 128):
                vbt = VNS if smp else VBt[tb % 2]
                vkey = "VNS" if smp else ("VBt", tb % 2)
                for i in range(2):
                    pv = PS[4 + i]
                    for c in range(8):
                        mm(pv[:, :], XB[:, c * 512 + tb * 128:c * 512 + tb * 128 + 128], WIN[sv[i]][:, c * 512:c * 512 + 512],
                           c == 0, c == 7, [("win", sv[i]), ("XB", c)], [PK[4 + i]])
                    vf = VF[i]
                    act(vf[:, :], pv[:, :], AF.Copy, [PK[4 + i]], [("VF", i)])
                    S_.dma(AQ, lambda e, vf=vf, tb=tb, i=i: e.dma_start(
                        out=vov[oc0 + tb * 128:oc0 + tb * 128 + 128, i * 512:(i + 1) * 512], in_=vf[:, :]), ("VF", i),
                        reads=[("VF", i)])
                    if smp:
                        dve(lambda e, vf=vf, i=i: e.tensor_copy(out=VNS[:, i * 512:(i + 1) * 512], in_=vf[:, :]),
                            [("VF", i)], [vkey])
                    else:
                        dve(lambda e, vbt=vbt, vf=vf, i=i: e.tensor_copy(
                            out=v3(vbt[:, :], 65)[:, i * 8:(i + 1) * 8, 0:64], in_=v3(vf[:, :], 64)), [("VF", i)], [vkey])
                if not smp:
                    kb = c0 // 128 + tb
                    S_.dma(AQ, lambda e, vbt=vbt, kb=kb: e.dma_start(
                        out=vBs.rearrange("h p k e -> p h k e")[:, :, kb, :], in_=v3(vbt[:, :], 65)), vkey, reads=[vkey],
                        writes=[("v_scr", kb)])
            if ksub <= 2:
                return
            slot = win_ld(lambda w: v3(w[:, 0:4096], 512)[:, :, 0:16], v3(wsc_ff[jf], 16), [("wsc_ff", jf)])
            pf = PS[0]
            for c in range(8):
                mm(pf[:, :N], WIN[slot][:, c * 512:c * 512 + 128], XB[:, c * 512:c * 512 + N], c == 0, c == 7,
                   [("win", slot), ("XB", c)], [PK[0]])
            E_, LOGF, Ft, R1, R2 = SG[0], T1[0], T2[0], SG[1], T1[1]
            act(E_[0:16, :N], pf[0:16, :N], AF.Exp, [PK[0], "NBF"], [("SG", 0)], bias=NBF[:, jf:jf + 1], scale=-1.0)
            act(E_[0:16, :N], E_[0:16, :N], AF.Ln, [("SG", 0), "ONEV"], [("SG", 0)], bias=ONEV[0:16, 0:1])
            dve(lambda e: e.tensor_scalar(out=LOGF[0:16, :N], in0=E_[0:16, :N], scalar1=-1.0, scalar2=None, op0=ALU.mult),
                [("SG", 0)], [("T1", 0)])
            lfo = (lfs_o if smp else lf_o)[jf]
            S_.dma(AQ, lambda e: e.dma_start(out=lfo[:, oc0:oc0 + N], in_=LOGF[0:16, :N]), ("T1", 0), reads=[("T1", 0)])
            if ksub <= 3:
                return
            if smp:
                for bs in range(NBS):
                    dve(lambda e, bs=bs: e.tensor_tensor_scan(
                        out=Ft[0:16, bs * TT:(bs + 1) * TT], data0=LOGF[0:16, bs * TT:(bs + 1) * TT],
                        data1=ZERB[0:16, 0:TT], initial=0.0, op0=ALU.add, op1=ALU.add), [("T1", 0), "ZERB"], [("T2", 0)])
            else:
                dve(lambda e: e.tensor_tensor_scan(out=Ft[0:16, :N], data0=LOGF[0:16, :N], data1=ZERB[0:16, :N],
                                                   initial=FLAST[0:16, 0:1], op0=ALU.add, op1=ALU.add),
                    [("T1", 0), "ZERB", "FLAST"], [("T2", 0)])
                dve(lambda e: e.tensor_copy(out=FLAST[0:16, 0:1], in_=Ft[0:16, N - 1:N]), [("T2", 0)], ["FLAST"])
            FA, NFA = PTT, OTT
            dve(lambda e: e.tensor_copy(out=FA[0:16, 0:N], in_=Ft[0:16, :N]), [("T2", 0)], ["PTT"])
            dve(lambda e: e.tensor_tensor(out=R1[0:16, :N], in0=Ft[0:16, :N], in1=FA[0:16, 0:N], op=ALU.subtract),
                [("T2", 0), "PTT"], [("SG", 1)])
            dve(lambda e: e.tensor_copy(out=FA[0:16, 512:512 + N], in_=R1[0:16, :N]), [("SG", 1)], ["PTT"])
            dve(lambda e: e.tensor_tensor(out=R2[0:16, :N], in0=R1[0:16, :N], in1=FA[0:16, 512:512 + N], op=ALU.subtract),
                [("SG", 1), "PTT"], [("T1", 1)])
            dve(lambda e: e.tensor_copy(out=FA[0:16, 1024:1024 + N], in_=R2[0:16, :N]), [("T1", 1)], ["PTT"])
            dve(lambda e: e.tensor_scalar(out=NFA[0:16, :], in0=FA[0:16, :], scalar1=-1.0, scalar2=None, op0=ALU.mult),
                ["PTT"], ["OTT"])
            fa3 = v3(FA[0:16, :], 512)[:, :, 0:N]
            nfa3 = v3(NFA[0:16, :], 512)[:, :, 0:N]
            on3 = v3(ONES3[0:16, :], 512)[:, :, 0:N]
            S_.dma(AQ, lambda e: e.dma_start(out=qTs[:, 64:67, c0:c0 + N], in_=fa3), "PTT", reads=["PTT"],
                   writes=[("aug_scr", 0, c0)])
            S_.dma(AQ, lambda e: e.dma_start(out=qTs[:, 67:70, c0:c0 + N], in_=on3), "ONES3", reads=["ONES3"],
                   writes=[("aug_scr", 1, c0)])
            S_.dma(AQ, lambda e: e.dma_start(out=kTs[:, 64:67, c0:c0 + N], in_=on3), "ONES3", reads=["ONES3"],
                   writes=[("aug_scr", 2, c0)])
            S_.dma(AQ, lambda e: e.dma_start(out=kTs[:, 67:70, c0:c0 + N], in_=nfa3), "OTT", reads=["OTT"],
                   writes=[("aug_scr", 3, c0)])

        def scr_cols_keys(cs):
            ks = []
            for c0 in cs:
                ks += [("qk_scr", 0, c0), ("qk_scr", 1, c0)] + [("aug_scr", i, c0) for i in range(4)]
            return ks

        def attention_prompt(jf):
            allc = [i * 512 for i in range(NTL)]
            rk = scr_cols_keys(allc) + [("v_scr", kb) for kb in range(NKB)]
            PTG = [PTT[:, 0:1024], PTT2[:, 0:1024], PTT3[:, 0:1024]]
            pending = [None]
            ob, okey = PS[6], PK[6]
            bcb, bkey = PS[7], PK[7]

            def finish2(h, qb):
                mm(bcb, ONESF[64:65, 0:128], RR[64:65, :], True, True, ["ONESF", "RR"], [bkey])
                ot = OTT[0:64, (qb % 2) * 512:(qb % 2) * 512 + 512]
                dve(lambda e, ot=ot: e.tensor_tensor(out=ot, in0=RR[0:64, :], in1=bcb[0:64, :], op=ALU.mult),
                    ["RR", bkey], [("OTTb", qb % 2)])
                S_.dma(AQ, lambda e, h=h, qb=qb, ot=ot: e.dma_start(out=oTs[h, :, qb * 512:qb * 512 + 512], in_=ot),
                       ("OTTb", qb % 2), reads=[("OTTb", qb % 2)], writes=[("o_scr", qb * 512)])

            for h in range(NH):
                S_.dma(AQ, lambda e, h=h: e.dma_start(out=B1[0:70, 0:S], in_=kTs[h, :, 0:S]), "B1ld", reads=rk,
                       writes=KS("B1", 0, S))
                S_.dma(AQ, lambda e, h=h: e.dma_start(out=B2[0:70, 0:S], in_=qTs[h, :, 0:S]), "B2ld", reads=rk,
                       writes=KS("B2", 0, S))
                S_.dma(AQ, lambda e, h=h: e.dma_start(out=XB[:, 0:NKB * 65], in_=vBs[h].rearrange("p k e -> p (k e)")),
                       "XBld", reads=rk, writes=KS("XB", 0, NKB * 65))
                for qb in range(NTL):
                    nk = 4 * (qb + 1)
                    groups = [list(range(i, min(i + 2, nk))) for i in range(0, nk, 2)]
                    ng = len(groups)

                    def emit_qk(gi):
                        grp = groups[gi]
                        gsel = gi % 3
                        for t, kb in enumerate(grp):
                            sl = PSG[gsel][:, t * 512:t * 512 + 512]
                            mm(sl, B1[0:70, kb * 128:kb * 128 + 128], B2[0:70, qb * 512:qb * 512 + 512], True, True,
                               KS("B1", kb * 128, kb * 128 + 128) + KS("B2", qb * 512, qb * 512 + 512), [PK[2 * gsel + t]])
                            if kb >= 4 * qb:
                                jm = kb - 4 * qb
                                dve(lambda e, sl=sl, jm=jm: e.tensor_tensor(
                                    out=sl, in0=sl, in1=MASK[:, jm * 512:jm * 512 + 512], op=ALU.add),
                                    [PK[2 * gsel + t], "MASK"], [PK[2 * gsel + t]])
                        n = len(grp)
                        act(PTG[gsel][:, 0:n * 512], PSG[gsel][:, 0:n * 512], AF.Exp, [PK[2 * gsel + t] for t in range(n)],
                            [("PTG", gsel)])

                    def emit_pv(gi):
                        grp = groups[gi]
                        gsel = gi % 3
                        for t, kb in enumerate(grp):
                            mm(ob[0:65, :], XB[:, kb * 65:kb * 65 + 65], PTG[gsel][:, t * 512:t * 512 + 512],
                               kb == 0, kb == nk - 1, KS("XB", kb * 65, kb * 65 + 65) + [("PTG", gsel)], [okey])

                    emit_qk(0)
                    if ng > 1:
                        emit_qk(1)
                    if pending[0] is not None:
                        pending[0]()
                        pending[0] = None
                    for gi in range(ng):
                        if gi + 2 < ng:
                            emit_qk(gi + 2)
                        if gi >= 2 and NDUMMY:
                            for _ in range(NDUMMY):
                                mm(bcb, B1[0:70, 0:128], B2[0:70, qb * 512:qb * 512 + 512], True, True,
                                   KS("B1", 0, 128) + KS("B2", qb * 512, qb * 512 + 512), [bkey])
                        emit_pv(gi)
                    dve(lambda e: e.tensor_copy(out=RR[0:65, :], in_=ob[0:65, :]), [okey], ["RR"])
                    dve(lambda e: e.reciprocal(out=RR[64:65, :], in_=RR[64:65, :]), ["RR"], ["RR"])
                    pending[0] = (lambda h=h, qb=qb: finish2(h, qb))
            if pending[0] is not None:
                pending[0]()
                pending[0] = None

        def attention_sample(jf):
            rk = scr_cols_keys([S])
            FC, CL, SS, RA = Z[:, 0:P], Z[:, 1024:1024 + P], Z[:, 2048:2048 + P], Z[:, 3072:3072 + P]
            zk = [("Z", i) for i in range(8)]
            S_.dma(AQ, lambda e: e.dma_start(out=CL, in_=clf[jf]), "Zld", writes=zk)
            dve(lambda e: e.tensor_tensor_scan(out=FC, data0=CL, data1=ZERB[:, 0:P], initial=0.0, op0=ALU.add, op1=ALU.add),
                zk + ["ZERB"], zk)
            dve(lambda e: e.tensor_scalar(out=SS, in0=FC, scalar1=Z[:, P - 1:P], scalar2=-1.0, op0=ALU.subtract, op1=ALU.mult),
                zk, zk)
            b2k = KS("B2", 0, 3 * P)
            dve(lambda e: e.tensor_copy(out=B2[:, 0:P], in_=SS), zk, b2k)
            dve(lambda e: e.tensor_tensor(out=RA, in0=SS, in1=B2[:, 0:P], op=ALU.subtract), zk + b2k, zk)
            dve(lambda e: e.tensor_copy(out=B2[:, P:2 * P], in_=RA), zk, b2k)
            dve(lambda e: e.tensor_tensor(out=CL, in0=RA, in1=B2[:, P:2 * P], op=ALU.subtract), zk + b2k, zk)
            dve(lambda e: e.tensor_copy(out=B2[:, 2 * P:3 * P], in_=CL), zk, b2k)
            S_.dma(AQ, lambda e: e.dma_start(out=ssS, in_=v3(B2[:, 0:3 * P], P)), "B2ld", reads=b2k, writes=["ss_scr"])
            KN0, QS0 = 4096, 6144
            S_.dma(AQ, lambda e: e.dma_start(out=v3(B2[0:70, KN0:KN0 + 2048], 128)[:, :, 0:NS],
                                               in_=kTs.rearrange("h p s -> p h s")[:, :, S:S + NS]), "B2ld",
                   reads=rk, writes=KS("B2", KN0, KN0 + 2048))
            S_.dma(AQ, lambda e: e.dma_start(out=v3(B2[0:70, QS0:QS0 + 2048], 128)[:, :, 0:NS],
                                               in_=qTs.rearrange("h p s -> p h s")[:, :, S:S + NS]), "B2ld",
                   reads=rk, writes=KS("B2", QS0, QS0 + 2048))
            b1k = KS("B1", 0, NH * P)
            dve(lambda e: e.memset(B1[64:70, 0:NH * P], 1.0), [], b1k)
            nkb = P // 128
            VC = WOT
            vck = [("wo", 0), ("wo", 1)]
            for bs in range(NBS):
                S_.dma("pool", lambda e, bs=bs: e.dma_start(out=v3(B1[0:64, 0:NH * P], P),
                                                            in_=ckT[jf, bs].rearrange("h d p -> d h p")), "B1ld", writes=b1k)
                S_.dma(AQ, lambda e, bs=bs: e.dma_start(out=v3(B1[67:70, 0:NH * P], P),
                                                          in_=ssS[bs * NH:(bs + 1) * NH].rearrange("h r p -> r h p")),
                       "B1ld", reads=["ss_scr"], writes=b1k)
                S_.dma("pool", lambda e, bs=bs: e.dma_start(out=v3(VC[:, 0:nkb * 1024], 1024),
                                                            in_=cv[jf, bs].rearrange("(k p) f -> p k f", p=128)),
                       "VCld", writes=vck)
                for h in range(NH):
                    sbk = PS[h % 2]
                    qrhs = B2[0:70, QS0 + h * 128 + bs * TT:QS0 + h * 128 + (bs + 1) * TT]
                    for kb in range(nkb):
                        mm(sbk[:, kb * TT:(kb + 1) * TT], B1[0:70, h * P + kb * 128:h * P + kb * 128 + 128], qrhs, True, True,
                           b1k + KS("B2", QS0, QS0 + 2048), [PK[h % 2]])
                    nc0 = nkb * TT
                    mm(sbk[:, nc0:nc0 + TT], B2[0:70, KN0 + h * 128:KN0 + h * 128 + 128], qrhs, True, True,
                       KS("B2", KN0, KN0 + 4096), [PK[h % 2]])
                    pt = PTT[:, (h % 3) * 512:(h % 3) * 512 + 512]
                    ptk = ("PTG", 0)
                    act(pt[:, 0:nc0], sbk[:, 0:nc0], AF.Exp, [PK[h % 2]], [ptk, ("sbser", h % 2)])
                    w3 = W3[h % 2]
                    dve(lambda e, w3=w3, sbk=sbk, bs=bs: e.tensor_tensor(
                        out=w3[:, 0:TT], in0=sbk[:, nc0:nc0 + TT], in1=MASKS[:, bs * TT:(bs + 1) * TT], op=ALU.add),
                        [PK[h % 2], "MASKS"], [("W3", h % 2), ("sbser", h % 2)])
                    act(pt[:, nc0:nc0 + TT], w3[:, 0:TT], AF.Exp, [("W3", h % 2)], [ptk])
                    ob, rsb = PS[4 + h % 2], PS[6 + h % 2]
                    hp, par = h // 2, h % 2
                    for kb in range(nkb):
                        mm(ob[:, 0:TT], VC[:, kb * 1024 + hp * 128:kb * 1024 + hp * 128 + 128], pt[:, kb * TT:(kb + 1) * TT],
                           kb == 0, False, vck + [ptk], [PK[4 + h % 2]])
                    mm(ob[:, 0:TT], VNS[:, hp * 128:hp * 128 + 128], pt[:, nc0:nc0 + TT], False, True, ["VNS", ptk],
                       [PK[4 + h % 2]])
                    for kb in range(nkb + 1):
                        mm(rsb[:, 0:TT], ONES128[:, :], pt[:, kb * TT:(kb + 1) * TT], kb == 0, kb == nkb, ["ONES128", ptk],
                           [PK[6 + h % 2]])
                    pp0 = par * 64
                    dve(lambda e, rsb=rsb, pp0=pp0: e.reciprocal(out=RRs[pp0:pp0 + 64, :], in_=rsb[pp0:pp0 + 64, 0:TT]),
                        [PK[6 + h % 2]], ["RRs"])
                    dve(lambda e, ob=ob, hp=hp, bs=bs, pp0=pp0: e.tensor_tensor(
                        out=OSALL[pp0:pp0 + 64, hp * 128 + bs * TT:hp * 128 + (bs + 1) * TT], in0=ob[pp0:pp0 + 64, 0:TT],
                        in1=RRs[pp0:pp0 + 64, :], op=ALU.mult), [PK[4 + h % 2], "RRs"], ["OSALL"])
            S_.dma(AQ, lambda e: e.dma_start(out=oTs.rearrange("(hp two) p s -> (two p) hp s", two=2)[:, :, S:S + NS],
                                               in_=v3(OSALL[:, :], 128)[:, :, 0:NS]), "OSALL", reads=["OSALL"],
                   writes=[("o_scr", S)])

        def fox_out(jf, tl):
            N, c0 = tl["N"], tl["c0"]
            S_.dma(AQ, lambda e: e.dma_start(out=v3(X[:, :], 512)[:, :, 0:N],
                                               in_=x1s.rearrange("(c p) s -> p c s", p=128)[:, :, c0:c0 + N]), "Xld",
                   reads=[("x1_scr", c0)], writes=[("X", c) for c in range(8)])
            S_.dma(AQ, lambda e: e.dma_start(out=v3(B1[:, 0:4096], 512)[:, :, 0:N],
                                               in_=oTs.rearrange("(hp two) p s -> (two p) hp s", two=2)[:, :, c0:c0 + N]),
                   "B1ld", reads=[("o_scr", c0)], writes=KS("B1", 0, 4096))
            for s2 in range(4):
                slot = wo_ld(2048, wsc_fo[jf, s2], [("wsc_fo", jf, s2)])
                for cc in range(2):
                    c = 2 * s2 + cc
                    py = PS[4 + c % 2]
                    for hp in range(8):
                        o0 = slot * 5632 + hp * 256 + cc * 128
                        mm(py[:, :N], WOT[:, o0:o0 + 128], B1[:, hp * 512:hp * 512 + N], hp == 0, hp == 7,
                           [("wo", slot), ("B1", hp)], [PK[4 + c % 2]])
                    dve(lambda e, c=c, py=py: e.scalar_tensor_tensor(
                        out=Z[:, c * 512:c * 512 + N], in0=X[:, c * 512:c * 512 + N], scalar=ALPHA, in1=py[:, :N],
                        op0=ALU.mult, op1=ALU.add), [("X", c), PK[4 + c % 2]], [("Z", c)])

        def poolmix(jp, l, tl, first, last):
            N, c0, smp = tl["N"], tl["c0"], tl["smp"]
            nseq, T = (NBS, TT) if smp else (1, N)
            L = 16 + T
            if smp:
                S_.dma(AQ, lambda e: e.dma_start(out=v3(SPOOL[:, :], NBS * 15),
                                                   in_=spool[jp].rearrange("(c p) b t -> p c (b t)", p=128)), "SPOOL",
                       writes=["SPOOL"])
                pov = pools_o[jp].rearrange("(c p) b t -> p c b t", p=128)
                for c in range(8):
                    S_.dma(AQ, lambda e, c=c: e.dma_start(out=pov[:, c, :, :],
                                                            in_=v3(X[:, c * 512:c * 512 + NS], TT)[:, :, 1:16]), "Xst",
                           reads=[("X", c)])
            elif last:
                S_.dma(AQ, lambda e: e.dma_start(out=pool_o[jp].rearrange("(c p) t -> p c t", p=128),
                                                   in_=v3(X[:, :], 512)[:, :, N - 15:N]), "Xst",
                       reads=[("X", c) for c in range(8)])
            slot = win_ld(lambda w: v3(w[:, 0:4096], 512)[:, :, 0:256], v3(wsc_pw[jp], 256), [("wsc_pw", jp)])
            WKv = [v3(W3[i][:, 0:nseq * L], L) for i in range(3)]
            wk = [("W3", i) for i in range(3)]
            for c in range(8):
                w = POOL_W[c]
                xc3 = v3(X[:, c * 512:c * 512 + nseq * T], T)
                if smp:
                    hsrc = v3(SPOOL[:, c * NBS * 15:(c + 1) * NBS * 15], 15)
                    hk = "SPOOL"
                else:
                    hsrc = v3(HIST[:, c * 16:(c + 1) * 16], 16)[:, :, 1:16]
                    hk = "HIST"
                dve(lambda e, hsrc=hsrc: e.tensor_copy(out=WKv[0][:, :, 1:16], in_=hsrc), [hk], [wk[0]])
                dve(lambda e, xc3=xc3: e.tensor_copy(out=WKv[0][:, :, 16:L], in_=xc3), [("X", c)], [wk[0]])
                if not smp:
                    dve(lambda e, c=c: e.tensor_copy(out=HIST[:, c * 16 + 1:c * 16 + 16], in_=X[:, c * 512 + N - 15:c * 512 + N]),
                        [("X", c)], ["HIST"])
                a, b_ = 0, 1
                sh = 1
                lo = 1
                while sh < w:
                    src, dst = WKv[a], WKv[b_]
                    lo2 = lo + sh
                    dve(lambda e, src=src, dst=dst, lo2=lo2, sh=sh: e.tensor_tensor(
                        out=dst[:, :, lo2:L], in0=src[:, :, lo2:L], in1=src[:, :, lo2 - sh:L - sh], op=ALU.add),
                        [wk[a]], [wk[b_]])
                    lo = lo2
                    sh *= 2
                    a, b_ = b_, (2 if b_ == 1 else 1)
                sw = WKv[a]
                plb = v3(B1[:, c * 512:c * 512 + nseq * T], T)
                dve(lambda e, sw=sw, plb=plb, xc3=xc3, w=w: e.scalar_tensor_tensor(
                    out=plb, in0=sw[:, :, 16:L], scalar=1.0 / w, in1=xc3, op0=ALU.mult, op1=ALU.subtract),
                    [wk[a], ("X", c)], [("B1", c)])
                if first and not smp:
                    t1 = T1[c % 2]
                    dve(lambda e, t1=t1, a=a, c=c: e.tensor_tensor(out=t1[:, 0:16], in0=W3[a][:, 16:32],
                                                                   in1=INVC[:, c * 16:(c + 1) * 16], op=ALU.mult),
                        [wk[a], "INVC"], [("T1", c % 2)])
                    dve(lambda e, t1=t1, c=c: e.tensor_tensor(out=B1[:, c * 512:c * 512 + 16], in0=t1[:, 0:16],
                                                              in1=X[:, c * 512:c * 512 + 16], op=ALU.subtract),
                        [("T1", c % 2), ("X", c)], [("B1", c)])
            for g in range(4):
                for dd in range(2):
                    co = 2 * g + dd
                    py = PS[4 + co % 2]
                    for cc in range(2):
                        ci = 2 * g + cc
                        mm(py[:, :N], WIN[slot][:, ci * 512 + dd * 128:ci * 512 + dd * 128 + 128], B1[:, ci * 512:ci * 512 + N],
                           cc == 0, cc == 1, [("win", slot), ("B1", ci)], [PK[4 + co % 2]])
                    t1 = T1[co % 2]
                    S_.op("act", lambda e, t1=t1, co=co: e.mul(out=t1[:, :N], in_=X[:, co * 512:co * 512 + N], mul=ALPHA),
                          reads=[("X", co)], writes=[("T1", co % 2)])
                    dve(lambda e, t1=t1, py=py, co=co: e.scalar_tensor_tensor(
                        out=Z[:, co * 512:co * 512 + N], in0=py[:, :N], scalar=PSC[:, jp * 8 + co:jp * 8 + co + 1],
                        in1=t1[:, :N], op0=ALU.mult, op1=ALU.add), [PK[4 + co % 2], ("T1", co % 2), "PSC"], [("Z", co)])

        def load_x(tl):
            N, c0 = tl["N"], tl["c0"]
            src = (xsT if tl["smp"] else xT).rearrange("(c p) s -> p c s", p=128)
            sc0 = 0 if tl["smp"] else c0
            S_.dma("sp" if (tl["first"] and not tl["smp"]) else AQ, lambda e: e.dma_start(out=v3(X[:, :], 512)[:, :, 0:N], in_=src[:, :, sc0:sc0 + N]), "Xld",
                   writes=[("X", c) for c in range(8)])
            for c in range(8):
                act(XB[:, c * 512:c * 512 + N], X[:, c * 512:c * 512 + N], AF.Copy, [("X", c)], [("XB", c)])

        def store_x1(tl):
            N, c0 = tl["N"], tl["c0"]
            S_.dma(AQ, lambda e: e.dma_start(out=x1s.rearrange("(c p) s -> p c s", p=128)[:, :, c0:c0 + N],
                                               in_=v3(X[:, :], 512)[:, :, 0:N]), "Xst", reads=[("X", c) for c in range(8)],
                   writes=[("x1_scr", c0)])

        def store_y(tl):
            N, c0 = tl["N"], tl["c0"]
            dst = (ysT if tl["smp"] else yT).rearrange("(c p) s -> p c s", p=128)
            oc0 = 0 if tl["smp"] else c0
            S_.dma(AQ, lambda e: e.dma_start(out=dst[:, :, oc0:oc0 + N], in_=v3(X[:, :], 512)[:, :, 0:N]), "Xst",
                   reads=[("X", c) for c in range(8)])

        tiles = [dict(N=512, c0=i * 512, smp=False, first=(i == 0), last=(i == NTL - 1)) for i in range(NTL)]
        tiles.append(dict(N=NS, c0=S, smp=True, first=True, last=True))

        def layer_tail(l, tl):
            ffn(l, 1, tl["N"])
            ple(l, tl)

        def run_layers_from(l, tl, started):
            while l < depth:
                if not started:
                    ffn(l, 0, tl["N"])
                    if l % 2 == 0:
                        store_x1(tl)
                        foxproj(l // 2, tl)
                        return
                    poolmix(l // 2, l, tl, tl["first"], tl["last"])
                    layer_norm(l * 4 + 1, tl["N"])
                layer_tail(l, tl)
                started = False
                l += 1
            store_y(tl)

        def reset_carry():
            dve(lambda e: e.memset(FLAST[:, :], 0.0), [], ["FLAST"])
            dve(lambda e: e.memset(HIST[:, :], 0.0), [], ["HIST"])

        import os
        dbg = int(os.environ.get("KDBG", "0"))
        reset_carry()
        load_x(tiles[0])
        for l_ in range(depth):
            cast_ffn(l_, 0)
            if l_ % 2 == 0:
                cast_fox_in(l_ // 2); cast_fox_o(l_ // 2)
            else:
                cast_pool(l_ // 2)
            cast_ffn(l_, 1)
            cast_ple(l_)
        if dbg:
            tl = tiles[0] if dbg < 20 else tiles[-1]
            d2 = dbg % 20
            load_x(tl)
            if d2 >= 2 and not os.environ.get("NOFFN"):
                ffn(0, 0, tl["N"])
            if d2 >= 3:
                store_x1(tl)
                foxproj(0, tl)
            if d2 >= 4:
                for t2 in tiles[1:]:
                    load_x(t2); store_x1(t2); foxproj(0, t2)
            if d2 >= 5:
                attention_prompt(0)
            if d2 >= 6:
                attention_sample(0)
            if d2 >= 7:
                fox_out(0, tl); layer_norm(1, tl["N"])
            if d2 >= 8:
                ple(0, tl)
            if d2 >= 9:
                poolmix(0, 1, tl, True, True); layer_norm(5, tl["N"])
            store_y(tl)
            depth = 0
        else:
          for tl in tiles:
            if not (tl["first"] and not tl["smp"]):
                load_x(tl)
            run_layers_from(0, tl, False)
        l = 0
        while l < depth:
            jf = l // 2
            attention_prompt(jf)
            attention_sample(jf)
            reset_carry()
            for tl in tiles:
                fox_out(jf, tl)
                layer_norm(l * 4 + 1, tl["N"])
                run_layers_from(l, tl, True)
            l += 2
        S_.emit(nc)
    return nc


NDUMMY = 1
REAL_CORES = [0, 1, 4, 5]


def _consts(NBS):
    p = np.arange(128)[:, None]
    f = np.arange(512)[None, :]
    maskp = np.concatenate([np.where(f >= j * 128 + p, 0.0, NEG) for j in range(4)], axis=1).astype(np.float32)
    q = np.arange(TT)[None, :]
    masks = np.concatenate([np.where((p // TT == bs) & (p % TT <= q), 0.0, NEG) for bs in range(NBS)], axis=1).astype(np.float32)
    invc = np.zeros((128, 128), np.float32)
    for c in range(8):
        w = POOL_W[c]
        invc[:, c * 16:(c + 1) * 16] = 1.0 / np.minimum(np.arange(16) + 1, w)
    return maskp, masks, invc


def _vec_layout(a):
    lead = int(np.prod(a.shape[:-1]))
    return np.ascontiguousarray(a.reshape(lead, 8, 128).transpose(2, 0, 1).reshape(128, lead * 8)).astype(np.float32)


def make_in_maps(inp, n_cores, NBS):
    f = lambda a: np.ascontiguousarray(np.asarray(a, dtype=np.float32))
    B = inp["x_prompt"].shape[0]
    maskp, masks, invc = _consts(NBS)
    shared = {
        "ffn_w_in": f(inp["ffn_w_in"]), "ffn_w_out": f(inp["ffn_w_out"]), "fox_w_in": f(inp["fox_w_in"]),
        "fox_w_o": f(inp["fox_w_o"]), "pool_w": f(inp["pool_w"]), "ple_w_proj": f(inp["ple_w_proj"]),
        "ple_w_gate": f(inp["ple_w_gate"]),
        "lng": _vec_layout(np.asarray(inp["ln_g"])), "lnb": _vec_layout(np.asarray(inp["ln_b"])),
        "bgate": _vec_layout(np.asarray(inp["ple_b_gate"])), "pscale": _vec_layout(np.asarray(inp["pool_scale"])),
        "bf": f(np.asarray(inp["fox_b_f"]).T), "maskp": maskp, "masks": masks, "invc": invc,
    }
    maps = []
    zero_map = None
    for core in range(n_cores):
        if core not in REAL_CORES:
            if zero_map is None:
                zero_map = {k: np.zeros_like(v) for k, v in maps[0].items()}
            maps.append(zero_map)
            continue
        b = REAL_CORES.index(core)
        sl = slice(b * NBS, (b + 1) * NBS)
        m = dict(shared)
        m["xT"] = f(np.asarray(inp["x_prompt"])[b].T)
        m["pT"] = f(np.asarray(inp["p_prompt"])[:, b].transpose(0, 2, 1))
        xs = np.asarray(inp["x_sample"])[sl]
        m["xsT"] = f(xs.reshape(NBS * TT, D).T)
        ps = np.asarray(inp["p_sample"])[:, sl]
        m["psT"] = f(ps.reshape(ps.shape[0], NBS * TT, PLE).transpose(0, 2, 1))
        m["ckT"] = f(np.asarray(inp["cache_fox_k"])[:, sl].transpose(0, 1, 3, 4, 2))
        ck = np.asarray(inp["cache_fox_v"])[:, sl]
        m["cv"] = f(ck.reshape(ck.shape[0], NBS, ck.shape[2], D))
        cl = np.asarray(inp["cache_fox_logf"])[:, sl].transpose(0, 1, 3, 2)
        m["clf"] = f(cl.reshape(cl.shape[0], NBS * NH, cl.shape[3]))
        m["spool"] = f(np.asarray(inp["state_pool"])[:, sl].transpose(0, 3, 1, 2))
        maps.append(m)
    return maps


def assemble(results, B, S, NBS):
    results = [results[c] for c in REAL_CORES[:B]]
    NF = results[0]["kT_o"].shape[0]
    NP = results[0]["pool_o"].shape[0]
    DB = B * NBS
    yp = np.stack([results[b]["yT"].T for b in range(B)])
    ys = np.concatenate([results[b]["ysT"].T.reshape(NBS, TT, D) for b in range(B)])
    kp = np.stack([results[b]["kT_o"].transpose(0, 3, 1, 2) for b in range(B)], axis=1)
    vp = np.stack([results[b]["v_o"].reshape(NF, S, NH, DH) for b in range(B)], axis=1)
    fp = np.stack([results[b]["lf_o"].transpose(0, 2, 1) for b in range(B)], axis=1)
    pp = np.stack([results[b]["pool_o"].transpose(0, 2, 1) for b in range(B)], axis=1)
    ks = np.concatenate([results[b]["kTs_o"].transpose(0, 3, 1, 2).reshape(NF, NBS, TT, NH, DH) for b in range(B)], axis=1)
    vs = np.concatenate([results[b]["vs_o"].reshape(NF, NBS, TT, NH, DH) for b in range(B)], axis=1)
    fs = np.concatenate([results[b]["lfs_o"].transpose(0, 2, 1).reshape(NF, NBS, TT, NH) for b in range(B)], axis=1)
    psm = np.concatenate([results[b]["pools_o"].transpose(0, 2, 3, 1) for b in range(B)], axis=1)
    outs = (yp, ys, kp, vp, fp, pp, ks, vs, fs, psm)
    return tuple(np.ascontiguousarray(o, dtype=np.float32) for o in outs)


def kernel(**inputs):
    B, S, _ = inputs["x_prompt"].shape
    DB = inputs["x_sample"].shape[0]
    P = inputs["cache_fox_k"].shape[2]
    NBS = DB // B
    n_cores = 8
    nc = build_program(S, NBS, P)
    in_maps = make_in_maps(inputs, n_cores, NBS)
    res = run_bass_kernel_spmd(nc, in_maps, core_ids=list(range(n_cores)))
    return assemble(res.results, B, S, NBS)
```

```python
import numpy as np
from contextlib import ExitStack
import concourse.bass as bass
import concourse.mybir as mybir
from concourse.bass_utils import run_bass_kernel_spmd

F32 = mybir.dt.float32
BF16 = mybir.dt.bfloat16
AF = mybir.ActivationFunctionType
ALU = mybir.AluOpType

D = 1024
NH = 16
DH = 64
DFF = 2816
PLE = 256
DEPTH = 4
TT = 16
ALPHA = (2.0 * DEPTH) ** 0.25
LN_EPS = 1e-5
POOL_W = (2, 2, 4, 4, 8, 8, 16, 16)
NEG = -1.0e30

ENGS = ("pe", "act", "dve", "pool", "sp")


class _Op:
    __slots__ = ("eng", "fn", "deps", "is_dma", "dkey", "idx", "has_dep", "inc_idx", "dma_val")

    def __init__(self, eng, fn, is_dma, dkey):
        self.eng = eng
        self.fn = fn
        self.deps = []
        self.is_dma = is_dma
        self.dkey = dkey
        self.has_dep = False
        self.inc_idx = None
        self.dma_val = None


class Sched:
    def __init__(self):
        self.ops = []
        self.last_w = {}
        self.readers = {}
        self.dma_last = {}
        self.dma_cnt = {}

    def _add(self, op, reads, writes):
        deps = set()
        for k in reads:
            w = self.last_w.get(k)
            if w is not None:
                deps.add(w)
        for k in writes:
            w = self.last_w.get(k)
            if w is not None:
                deps.add(w)
            for r in self.readers.get(k, ()):
                deps.add(r)
        deps.discard(op)
        op.deps = list(deps)
        for d in op.deps:
            d.has_dep = True
        for k in reads:
            self.readers.setdefault(k, []).append(op)
        for k in writes:
            self.last_w[k] = op
            self.readers[k] = []
        op.idx = len(self.ops)
        self.ops.append(op)
        return op

    def op(self, eng, fn, reads=(), writes=()):
        return self._add(_Op(eng, fn, False, None), list(reads), list(writes))

    def dma(self, eng, fn, dkey, reads=(), writes=()):
        op = _Op(eng, fn, True, dkey)
        prev = self.dma_last.get(dkey)
        self._add(op, list(reads), list(writes))
        if prev is not None and prev not in op.deps:
            op.deps.append(prev)
            prev.has_dep = True
        self.dma_last[dkey] = op
        self.dma_cnt[dkey] = self.dma_cnt.get(dkey, 0) + 1
        op.dma_val = 16 * self.dma_cnt[dkey]
        return op

    def emit(self, nc, final_wait_eng="sp"):
        dkeys = list(self.dma_cnt.keys())
        cnt = {e: 0 for e in ENGS}
        for o in self.ops:
            if not o.is_dma and o.has_dep:
                cnt[o.eng] += 1
                o.inc_idx = cnt[o.eng]
        with ExitStack() as es:
            esem = {e: es.enter_context(nc.semaphore("s_" + e)) for e in ENGS}
            dsem = {k: es.enter_context(nc.semaphore("d_%d" % i)) for i, k in enumerate(dkeys)}
            block = es.enter_context(nc.Block())
            per_eng = {e: [o for o in self.ops if o.eng == e] for e in ENGS}

            def run_stream(e, engobj):
                known = {}
                for o in per_eng[e]:
                    need = {}
                    for d in o.deps:
                        if d.is_dma:
                            s, v = dsem[d.dkey], d.dma_val
                        else:
                            if d.eng == "pe" and e == "pe" and not o.is_dma:
                                continue
                            s, v = esem[d.eng], d.inc_idx
                        if need.get(s, 0) < v:
                            need[s] = v
                    for s, v in need.items():
                        if known.get(s, 0) < v:
                            engobj.wait_ge(s, v)
                            known[s] = v
                    ins = o.fn(engobj)
                    if o.is_dma:
                        ins.then_inc(dsem[o.dkey], 16)
                    elif o.inc_idx is not None:
                        ins.then_inc(esem[e], 1)
                if e == final_wait_eng:
                    for k in dkeys:
                        engobj.wait_ge(dsem[k], 16 * self.dma_cnt[k])
                    for e2 in ENGS:
                        if cnt[e2] > 0:
                            engobj.wait_ge(esem[e2], cnt[e2])

            @block.tensor
            def _(eng):
                run_stream("pe", eng)

            @block.scalar
            def _(eng):
                run_stream("act", eng)

            @block.vector
            def _(eng):
                run_stream("dve", eng)

            @block.gpsimd
            def _(eng):
                run_stream("pool", eng)

            @block.sync
            def _(eng):
                run_stream("sp", eng)


def KS(name, c0, c1, g=512):
    return [(name, b) for b in range(c0 // g, (c1 - 1) // g + 1)]


def build_program(S, NBS, P, depth=DEPTH):
    NS = NBS * TT
    SC = S + NS
    NKB = S // 128
    NTL = S // 512
    NFOX = (depth + 1) // 2
    NPOOL = depth // 2
    assert NS <= 128 and S % 512 == 0 and P % 128 == 0 and P <= 1024
    nc = bass.Bass("TRN2", target_bir_lowering=False)

    def din(name, shape, dt=F32):
        return nc.dram_tensor(name, list(shape), dt, kind="ExternalInput").ap()

    def dout(name, shape, dt=F32):
        return nc.dram_tensor(name, list(shape), dt, kind="ExternalOutput").ap()

    def dscr(name, shape, dt):
        return nc.dram_tensor(name, list(shape), dt, kind="Internal").ap()

    xT = din("xT", [D, S]); pT = din("pT", [depth, PLE, S])
    xsT = din("xsT", [D, NS]); psT = din("psT", [depth, PLE, NS])
    ckT = din("ckT", [NFOX, NBS, NH, DH, P]); cv = din("cv", [NFOX, NBS, P, D])
    clf = din("clf", [NFOX, NBS * NH, P]); spool = din("spool", [NPOOL, D, NBS, 15])
    w_in = din("ffn_w_in", [depth, 2, D, 2 * DFF]); w_out = din("ffn_w_out", [depth, 2, DFF, D])
    fw_in = din("fox_w_in", [NFOX, D, 3 * D + NH]); fw_o = din("fox_w_o", [NFOX, D, D])
    pw = din("pool_w", [NPOOL, 4, 256, 256])
    wproj = din("ple_w_proj", [depth, PLE, D]); wgate = din("ple_w_gate", [depth, D, D])
    lng_d = din("lng", [128, depth * 32]); lnb_d = din("lnb", [128, depth * 32])
    bg_d = din("bgate", [128, depth * 8]); psc_d = din("pscale", [128, NPOOL * 8])
    bf_d = din("bf", [16, NFOX])
    maskp_d = din("maskp", [128, 2048]); masks_d = din("masks", [128, NBS * TT])
    invc_d = din("invc", [128, 128])

    yT = dout("yT", [D, S]); ysT = dout("ysT", [D, NS])
    kT_o = dout("kT_o", [NFOX, NH, DH, S]); v_o = dout("v_o", [NFOX, S, D]); lf_o = dout("lf_o", [NFOX, NH, S])
    pool_o = dout("pool_o", [NPOOL, D, 15])
    kTs_o = dout("kTs_o", [NFOX, NH, DH, NS]); vs_o = dout("vs_o", [NFOX, NS, D]); lfs_o = dout("lfs_o", [NFOX, NH, NS])
    pools_o = dout("pools_o", [NPOOL, D, NBS, 15])

    qTs = dscr("qTs", [NH, 70, SC], BF16); kTs = dscr("kTs", [NH, 70, SC], BF16)
    vBs = dscr("vBs", [NH, 128, NKB, 65], BF16); oTs = dscr("oTs", [NH, DH, SC], BF16)
    x1s = dscr("x1s", [D, SC], F32); ssS = dscr("ssS", [NBS * NH, 3, P], BF16)
    wsc_in = dscr("wsc_in", [depth, 2, 11, 128, 4096], BF16)
    wsc_out = dscr("wsc_out", [depth, 2, 4, 128, 5632], BF16)
    wsc_fq = dscr("wsc_fq", [NFOX, 6, 128, 4096], BF16)
    wsc_ff = dscr("wsc_ff", [NFOX, 128, 128], BF16)
    wsc_fo = dscr("wsc_fo", [NFOX, 4, 128, 2048], BF16)
    wsc_pw = dscr("wsc_pw", [NPOOL, 128, 2048], BF16)
    wsc_g = dscr("wsc_g", [depth, 2, 128, 4096], BF16)
    wsc_p = dscr("wsc_p", [depth, 2, 128, 1024], BF16)

    S_ = Sched()
    es = ExitStack()
    with es:
        def sb(name, shape, dt):
            return es.enter_context(nc.sbuf_tensor(name, list(shape), dt))

        X = sb("X", [128, 4096], F32)
        XB = sb("XB", [128, 4224], BF16)
        B1 = sb("B1", [128, 16384], BF16)
        Z = sb("Z", [128, 4096], F32)
        B2 = sb("B2", [128, 8192], BF16)
        MEANS = sb("MEANS", [128, 512], F32); VAR = sb("VAR", [128, 512], F32); RSTD = sb("RSTD", [128, 512], F32)
        T1 = [sb("T1_%d" % i, [128, 512], F32) for i in range(2)]
        T2 = [sb("T2_%d" % i, [128, 512], F32) for i in range(2)]
        SG = [sb("SG_%d" % i, [128, 512], F32) for i in range(2)]
        NWIN = 3
        WIN = [sb("WIN_%d" % i, [128, 4096], BF16) for i in range(NWIN)]
        WOT = sb("WOT", [128, 2 * 5632], BF16)
        VF = [sb("VF_%d" % i, [128, 512], F32) for i in range(2)]
        VBt = [sb("VBt_%d" % i, [128, 1040], BF16) for i in range(2)]
        VNS = sb("VNS", [128, 1024], BF16)
        PBt = sb("PBt", [128, 1024], BF16)
        PTT = sb("PTT", [128, 1536], BF16)
        OTT = sb("OTT", [128, 1536], BF16)
        ONES3 = sb("ONES3", [16, 1536], BF16)
        ZERB = sb("ZERB", [128, 1024], F32)
        FLAST = sb("FLAST", [16, 1], F32)
        W3 = [sb("W3_%d" % i, [128, 528], F32) for i in range(3)]
        HIST = sb("HIST", [128, 128], F32)
        SPOOL = sb("SPOOL", [128, 8 * NBS * 15], F32)
        RR = sb("RR", [65, 512], F32); BCs = sb("BCs", [64, 512], F32)
        RRs = sb("RRs", [128, 16], F32)
        OSALL = sb("OSALL", [128, 1024], BF16)
        MASK = sb("MASK", [128, 2048], BF16); MASKS = sb("MASKS", [128, NBS * TT], F32)
        INVC = sb("INVC", [128, 128], F32)
        ONESB = sb("ONESB", [128, 128], BF16); ONES128 = sb("ONES128", [128, 128], BF16)
        ONESF = sb("ONESF", [65, 128], F32)
        LNG = sb("LNG", [128, depth * 32], F32); LNB = sb("LNB", [128, depth * 32], F32)
        BG = sb("BG", [128, depth * 8], F32); PSC = sb("PSC", [128, NPOOL * 8], F32)
        NBF = sb("NBF", [16, NFOX], F32); EPSV = sb("EPSV", [128, 1], F32); ONEV = sb("ONEV", [128, 1], F32)
        PSG = [es.enter_context(nc.psum_tensor("pg%d" % i, [128, 1024], F32)) for i in range(3)]
        PS67 = [es.enter_context(nc.psum_tensor("pb%d" % i, [128, 512], F32)) for i in (6, 7)]
        PS = [PSG[i // 2][:, (i % 2) * 512:(i % 2) * 512 + 512] for i in range(6)] + [t[:, :] for t in PS67]
        PTT2 = sb("PTT2", [128, 1024], BF16)
        PTT3 = sb("PTT3", [128, 1024], BF16)
        PK = [("ps", i) for i in range(8)]

        def v3(ap, inner):
            return ap.rearrange("p (a b) -> p a b", b=inner)

        AQ = "pool"

        def ld(eng, out, in_, key):
            S_.dma(eng, lambda e: e.dma_start(out=out, in_=in_), key, writes=[key])

        ld("sp", LNG[:, :], lng_d, "LNG"); ld("sp", LNB[:, :], lnb_d, "LNB")
        ld("sp", BG[:, :], bg_d, "BG"); ld("sp", PSC[:, :], psc_d, "PSC")
        ld("sp", NBF[:, :], bf_d, "NBF"); ld("sp", MASKS[:, :], masks_d, "MASKS")
        ld("sp", INVC[:, :], invc_d, "INVC"); ld("pool", MASK[:, :], maskp_d, "MASK")
        S_.op("dve", lambda e: e.tensor_scalar(out=NBF[:, :], in0=NBF[:, :], scalar1=-1.0, scalar2=None, op0=ALU.mult),
              reads=["NBF"], writes=["NBF"])
        for t, val, key in ((ONES3, 1.0, "ONES3"), (ZERB, 0.0, "ZERB"), (ONESB, 1.0 / 1024.0, "ONESB"),
                            (ONES128, 1.0, "ONES128"), (ONESF, 1.0, "ONESF"), (EPSV, LN_EPS, "EPSV"),
                            (ONEV, 1.0, "ONEV"), (RR, 1.0, "RR"), (HIST, 0.0, "HIST")):
            S_.op("dve", lambda e, t=t, val=val: e.memset(t[:, :], val), writes=[key])
        for i in range(2):
            S_.op("dve", lambda e, i=i: e.memset(VBt[i][:, :], 1.0), writes=[("VBt", i)])
        S_.op("dve", lambda e: e.memset(VNS[:, :], 1.0), writes=["VNS"])
        for i in range(3):
            S_.op("dve", lambda e, i=i: e.memset(W3[i][:, :], 0.0), writes=[("W3", i)])

        ring = {"win": 0, "wo": 0}

        ncast = [0]

        def cast(dst, src, key):
            k = ("cast", ncast[0] % 8)
            ncast[0] += 1
            S_.dma("pool", lambda e: e.dma_start(out=dst, in_=src), k, writes=[key])

        def cast_ffn(l, k):
            wi = w_in[l, k].rearrange("(c p) f -> p c f", p=128)
            wo = w_out[l, k].rearrange("(j p) d -> p j d", p=128)
            for s in range(11):
                cast(v3(wsc_in[l, k, s], 512)[:, :, 0:256], wi[:, :, s * 256:(s + 1) * 256], ("wsc_in", l, k, s, 0))
                cast(v3(wsc_in[l, k, s], 512)[:, :, 256:512], wi[:, :, DFF + s * 256:DFF + (s + 1) * 256], ("wsc_in", l, k, s, 1))
            for s2 in range(4):
                cast(v3(wsc_out[l, k, s2], 256), wo[:, :, s2 * 256:(s2 + 1) * 256], ("wsc_out", l, k, s2))

        def cast_fox_in(jf):
            fw = fw_in[jf].rearrange("(c p) f -> p c f", p=128)
            for i in range(6):
                cast(v3(wsc_fq[jf, i], 512), fw[:, :, i * 512:(i + 1) * 512], ("wsc_fq", jf, i))
            cast(v3(wsc_ff[jf], 16), fw[:, :, 3072:3088], ("wsc_ff", jf))

        def cast_fox_o(jf):
            wov = fw_o[jf].rearrange("(c p) d -> p c d", p=128)
            for s2 in range(4):
                cast(v3(wsc_fo[jf, s2], 256), wov[:, :, s2 * 256:(s2 + 1) * 256], ("wsc_fo", jf, s2))

        def cast_ple(l):
            wg = wgate[l].rearrange("(c p) d -> p c d", p=128)
            wp = wproj[l].rearrange("(k p) d -> p k d", p=128)
            for s in range(2):
                cast(v3(wsc_g[l, s], 512), wg[:, :, s * 512:(s + 1) * 512], ("wsc_g", l, s))
                cast(v3(wsc_p[l, s], 512), wp[:, :, s * 512:(s + 1) * 512], ("wsc_p", l, s))

        def cast_pool(jp):
            cast(v3(wsc_pw[jp], 256), pw[jp].rearrange("g (cc p) d -> p (g cc) d", p=128), ("wsc_pw", jp))

        def win_ld(dst_of_slot, src, keys):
            slot = ring["win"] % NWIN
            ring["win"] += 1
            dst = dst_of_slot(WIN[slot])
            S_.dma("sp", lambda e: e.dma_start(out=dst, in_=src), ("win", slot), reads=keys, writes=[("win", slot)])
            return slot

        def wo_ld(ncols, src, keys):
            slot = ring["wo"] % 2
            ring["wo"] += 1
            dst = WOT[:, slot * 5632:slot * 5632 + ncols]
            S_.dma("sp", lambda e: e.dma_start(out=dst, in_=src), ("wo", slot), reads=keys, writes=[("wo", slot)])
            return slot

        def mm(out, lhsT, rhs, start, stop, reads, writes):
            S_.op("pe", lambda e: e.matmul(out, lhsT=lhsT, rhs=rhs, start=start, stop=stop), reads=reads, writes=writes)

        def act(out, in_, func, reads, writes, bias=None, scale=1.0):
            if bias is None:
                S_.op("act", lambda e: e.activation(out=out, in_=in_, func=func, scale=scale), reads=reads, writes=writes)
            else:
                S_.op("act", lambda e: e.activation(out=out, in_=in_, func=func, bias=bias, scale=scale),
                      reads=reads, writes=writes)

        def dve(fn, reads, writes):
            S_.op("dve", fn, reads=reads, writes=writes)

        def layer_norm(gi, N):
            psm, psq = PS[6], PS[7]
            for c in range(8):
                zc = Z[:, c * 512:c * 512 + N]
                act(B2[:, c * 512:c * 512 + N], zc, AF.Copy, [("Z", c)], [("B2", c)])
                act(B2[:, 4096 + c * 512:4096 + c * 512 + N], zc, AF.Square, [("Z", c)], [("B2", 8 + c)])
                mm(psm[:, :N], ONESB[:, :], B2[:, c * 512:c * 512 + N], c == 0, c == 7, [("B2", c), "ONESB"], [PK[6]])
                mm(psq[:, :N], ONESB[:, :], B2[:, 4096 + c * 512:4096 + c * 512 + N], c == 0, c == 7,
                   [("B2", 8 + c), "ONESB"], [PK[7]])
            act(MEANS[:, :N], psm[:, :N], AF.Copy, [PK[6]], ["MEANS"])
            act(RSTD[:, :N], psm[:, :N], AF.Square, [PK[6]], ["RSTD"])
            dve(lambda e: e.tensor_tensor(out=VAR[:, :N], in0=psq[:, :N], in1=RSTD[:, :N], op=ALU.subtract),
                [PK[7], "RSTD"], ["VAR"])
            act(VAR[:, :N], VAR[:, :N], AF.Sqrt, ["VAR", "EPSV"], ["VAR"], bias=EPSV[:, 0:1])
            dve(lambda e: e.reciprocal(out=RSTD[:, :N], in_=VAR[:, :N]), ["VAR"], ["RSTD"])
            for c in range(8):
                t1, t2 = T1[c % 2], T2[c % 2]
                zc = Z[:, c * 512:c * 512 + N]
                g = LNG[:, gi * 8 + c:gi * 8 + c + 1]
                b = LNB[:, gi * 8 + c:gi * 8 + c + 1]
                dve(lambda e, t1=t1, zc=zc: e.tensor_tensor(out=t1[:, :N], in0=zc, in1=MEANS[:, :N], op=ALU.subtract),
                    [("Z", c), "MEANS"], [("T1", c % 2)])
                dve(lambda e, t1=t1, t2=t2, g=g: e.scalar_tensor_tensor(out=t2[:, :N], in0=t1[:, :N], scalar=g,
                                                                        in1=RSTD[:, :N], op0=ALU.mult, op1=ALU.mult),
                    [("T1", c % 2), "RSTD", "LNG"], [("T2", c % 2)])
                act(XB[:, c * 512:c * 512 + N], t2[:, :N], AF.Identity, [("T2", c % 2), "LNB"], [("XB", c)], bias=b)
                act(X[:, c * 512:c * 512 + N], t2[:, :N], AF.Identity, [("T2", c % 2), "LNB"], [("X", c)], bias=b)

        def ffn(l, k, N):
            def evac(j, pa, pu, ka, ku):
                sg = SG[j % 2]
                act(sg[:, :N], pa[:, :N], AF.Silu, [ka], [("SG", j % 2)])
                dve(lambda e, sg=sg, pu=pu, j=j: e.scalar_tensor_tensor(
                    out=B1[:, j * 512:j * 512 + N], in0=sg[:, :N], scalar=0.5, in1=pu[:, :N], op0=ALU.mult, op1=ALU.mult),
                    [("SG", j % 2), ku], [("B1", j)])

            def wcol(jj, u):
                return (256 if u else 0) + jj * 128

            slots = {}
            for s in range(2):
                slots[s] = win_ld(lambda w: w[:, 0:4096], wsc_in[l, k, s], [("wsc_in", l, k, s, 0), ("wsc_in", l, k, s, 1)])
            first = [(0, 0, 0, 0, 2), (1, 0, 1, 1, 3), (2, 1, 0, 4, 5)]
            for c in range(8):
                for (j, s, jj, ba, bu) in first:
                    for u, bnk in ((0, ba), (1, bu)):
                        o0 = c * 512 + wcol(jj, u)
                        mm(PS[bnk][:, :N], WIN[slots[s]][:, o0:o0 + 128], XB[:, c * 512:c * 512 + N], c == 0, c == 7,
                           [("win", slots[s]), ("XB", c)], [PK[bnk]])
            for (j, s, jj, ba, bu) in first:
                evac(j, PS[ba], PS[bu], PK[ba], PK[bu])
            for s in range(1, 11):
                if s >= 2:
                    slots[s] = win_ld(lambda w: w[:, 0:4096], wsc_in[l, k, s], [("wsc_in", l, k, s, 0), ("wsc_in", l, k, s, 1)])
                slot = slots[s]
                for jj in range(2):
                    j = 2 * s + jj
                    if j < 3:
                        continue
                    pa, pu = PS[j % 2], PS[2 + j % 2]
                    for c in range(8):
                        mm(pa[:, :N], WIN[slot][:, c * 512 + jj * 128:c * 512 + jj * 128 + 128], XB[:, c * 512:c * 512 + N],
                           c == 0, c == 7, [("win", slot), ("XB", c)], [PK[j % 2]])
                    for c in range(8):
                        mm(pu[:, :N], WIN[slot][:, c * 512 + 256 + jj * 128:c * 512 + 256 + jj * 128 + 128],
                           XB[:, c * 512:c * 512 + N], c == 0, c == 7, [("win", slot), ("XB", c)], [PK[2 + j % 2]])
                    evac(j, pa, pu, PK[j % 2], PK[2 + j % 2])
            for s2 in range(4):
                slot = wo_ld(5632, wsc_out[l, k, s2], [("wsc_out", l, k, s2)])
                for cc in range(2):
                    c = 2 * s2 + cc
                    py = PS[4 + c % 2]
                    for j in range(22):
                        o0 = slot * 5632 + j * 256 + cc * 128
                        mm(py[:, :N], WOT[:, o0:o0 + 128], B1[:, j * 512:j * 512 + N], j == 0, j == 21,
                           [("wo", slot), ("B1", j)], [PK[4 + c % 2]])
                    dve(lambda e, c=c, py=py: e.scalar_tensor_tensor(
                        out=Z[:, c * 512:c * 512 + N], in0=X[:, c * 512:c * 512 + N], scalar=ALPHA, in1=py[:, :N],
                        op0=ALU.mult, op1=ALU.add), [("X", c), PK[4 + c % 2]], [("Z", c)])
            layer_norm(l * 4 + 2 * k, N)

        def ple(l, tl):
            N, c0 = tl["N"], tl["c0"]
            psrc = (psT if tl["smp"] else pT)[l].rearrange("(k p) s -> p k s", p=128)
            pcol = 0 if tl["smp"] else c0
            S_.dma("pool", lambda e: e.dma_start(out=v3(PBt[:, :], 512)[:, :, 0:N], in_=psrc[:, :, pcol:pcol + N]),
                   "PBt", writes=["PBt"])
            for s in range(2):
                sg_ = win_ld(lambda w: w[:, 0:4096], wsc_g[l, s], [("wsc_g", l, s)])
                sp_ = win_ld(lambda w: w[:, 0:1024], wsc_p[l, s], [("wsc_p", l, s)])
                for cc in range(4):
                    c = 4 * s + cc
                    pg, pp = PS[c % 2], PS[2 + c % 2]
                    for kc in range(8):
                        mm(pg[:, :N], WIN[sg_][:, kc * 512 + cc * 128:kc * 512 + cc * 128 + 128], XB[:, kc * 512:kc * 512 + N],
                           kc == 0, kc == 7, [("win", sg_), ("XB", kc)], [PK[c % 2]])
                    for kk in range(2):
                        mm(pp[:, :N], WIN[sp_][:, kk * 512 + cc * 128:kk * 512 + cc * 128 + 128], PBt[:, kk * 512:kk * 512 + N],
                           kk == 0, kk == 1, [("win", sp_), "PBt"], [PK[2 + c % 2]])
                    sg = SG[c % 2]
                    act(sg[:, :N], pg[:, :N], AF.Sigmoid, [PK[c % 2], "BG"], [("SG", c % 2)], bias=BG[:, l * 8 + c:l * 8 + c + 1])
                    t1 = T1[c % 2]
                    dve(lambda e, sg=sg, pp=pp, t1=t1: e.tensor_tensor(out=t1[:, :N], in0=sg[:, :N], in1=pp[:, :N], op=ALU.mult),
                        [("SG", c % 2), PK[2 + c % 2]], [("T1", c % 2)])
                    dve(lambda e, c=c, t1=t1: e.scalar_tensor_tensor(
                        out=Z[:, c * 512:c * 512 + N], in0=X[:, c * 512:c * 512 + N], scalar=ALPHA, in1=t1[:, :N],
                        op0=ALU.mult, op1=ALU.add), [("X", c), ("T1", c % 2)], [("Z", c)])
            layer_norm(l * 4 + 3, N)

        def foxproj(jf, tl):
            N, c0, smp = tl["N"], tl["c0"], tl["smp"]
            fw = fw_in[jf].rearrange("(c p) f -> p c f", p=128)
            qv = qTs.rearrange("h p s -> p h s")
            kv = kTs.rearrange("h p s -> p h s")
            kov = (kTs_o if smp else kT_o)[jf].rearrange("h p s -> p h s")
            oc0 = 0 if smp else c0
            import os
            ksub = int(os.environ.get("KSUB", "9"))
            if ksub <= 0:
                return
            qv2 = qTs.rearrange("(hp two) p s -> two p hp s", two=2)
            kv2 = kTs.rearrange("(hp two) p s -> two p hp s", two=2)
            kov2 = (kTs_o if smp else kT_o)[jf].rearrange("(hp two) p s -> two p hp s", two=2)
            for which in range(2):
                for sq in range(2):
                    slot = win_ld(lambda w: w[:, 0:4096], wsc_fq[jf, which * 2 + sq], [("wsc_fq", jf, which * 2 + sq)])
                    stg0 = which * 4096
                    for pr in range(4):
                        pq = PS[pr % 2]
                        for c in range(8):
                            mm(pq[:, :N], WIN[slot][:, c * 512 + pr * 128:c * 512 + pr * 128 + 128], XB[:, c * 512:c * 512 + N],
                               c == 0, c == 7, [("win", slot), ("XB", c)], [PK[pr % 2]])
                        if which == 0:
                            act(B2[:, stg0 + pr * 512:stg0 + pr * 512 + N], pq[:, :N], AF.Copy, [PK[pr % 2]],
                                [("B2", which * 8 + pr)], scale=0.125)
                        else:
                            act(Z[:, pr * 512:pr * 512 + N], pq[:, :N], AF.Copy, [PK[pr % 2]], [("Z", pr)])
                            dve(lambda e, pr=pr: e.tensor_copy(out=B2[:, stg0 + pr * 512:stg0 + pr * 512 + N],
                                                               in_=Z[:, pr * 512:pr * 512 + N]),
                                [("Z", pr)], [("B2", which * 8 + pr)])
                    rk = [("B2", which * 8 + i) for i in range(4)]
                    for two in range(2):
                        stg = v3(B2[two * 64:two * 64 + 64, stg0:stg0 + 2048], 512)[:, :, 0:N]
                        dstv = (qv2 if which == 0 else kv2)[two, 0:64, sq * 4:(sq + 1) * 4, c0:c0 + N]
                        S_.dma(AQ, lambda e, dstv=dstv, stg=stg: e.dma_start(out=dstv, in_=stg), ("B2st", which), reads=rk,
                               writes=[("qk_scr", which, c0)])
                        if which == 1:
                            kf = v3(Z[two * 64:two * 64 + 64, 0:2048], 512)[:, :, 0:N]
                            dk = kov2[two, :, sq * 4:(sq + 1) * 4, oc0:oc0 + N]
                            S_.dma(AQ, lambda e, dk=dk, kf=kf: e.dma_start(out=dk, in_=kf), "Zst",
                                   reads=[("Z", i) for i in range(4)])
            if ksub <= 1:
                return
            sv = [win_ld(lambda w: w[:, 0:4096], wsc_fq[jf, 4 + i], [("wsc_fq", jf, 4 + i)]) for i in range(2)]
            vov = (vs_o if smp else v_o)[jf]
            for tb in range(N // 128):
                vbt = VNS if smp else VBt[tb % 2]
                vkey = "VNS" if smp else ("VBt", tb % 2)
                for i in range(2):
                    pv = PS[4 + i]
                    for c in range(8):
                        mm(pv[:, :], XB[:, c * 512 + tb * 128:c * 512 + tb * 128 + 128], WIN[sv[i]][:, c * 512:c * 512 + 512],
                           c == 0, c == 7, [("win", sv[i]), ("XB", c)], [PK[4 + i]])
                    vf = VF[i]
                    act(vf[:, :], pv[:, :], AF.Copy, [PK[4 + i]], [("VF", i)])
                    S_.dma(AQ, lambda e, vf=vf, tb=tb, i=i: e.dma_start(
                        out=vov[oc0 + tb * 128:oc0 + tb * 128 + 128, i * 512:(i + 1) * 512], in_=vf[:, :]), ("VF", i),
                        reads=[("VF", i)])
                    if smp:
                        dve(lambda e, vf=vf, i=i: e.tensor_copy(out=VNS[:, i * 512:(i + 1) * 512], in_=vf[:, :]),
                            [("VF", i)], [vkey])
                    else:
                        dve(lambda e, vbt=vbt, vf=vf, i=i: e.tensor_copy(
                            out=v3(vbt[:, :], 65)[:, i * 8:(i + 1) * 8, 0:64], in_=v3(vf[:, :], 64)), [("VF", i)], [vkey])
                if not smp:
                    kb = c0 // 128 + tb
                    S_.dma(AQ, lambda e, vbt=vbt, kb=kb: e.dma_start(
                        out=vBs.rearrange("h p k e -> p h k e")[:, :, kb, :], in_=v3(vbt[:, :], 65)), vkey, reads=[vkey],
                        writes=[("v_scr", kb)])
            if ksub <= 2:
                return
            slot = win_ld(lambda w: v3(w[:, 0:4096], 512)[:, :, 0:16], v3(wsc_ff[jf], 16), [("wsc_ff", jf)])
            pf = PS[0]
            for c in range(8):
                mm(pf[:, :N], WIN[slot][:, c * 512:c * 512 + 128], XB[:, c * 512:c * 512 + N], c == 0, c == 7,
                   [("win", slot), ("XB", c)], [PK[0]])
            E_, LOGF, Ft, R1, R2 = SG[0], T1[0], T2[0], SG[1], T1[1]
            act(E_[0:16, :N], pf[0:16, :N], AF.Exp, [PK[0], "NBF"], [("SG", 0)], bias=NBF[:, jf:jf + 1], scale=-1.0)
            act(E_[0:16, :N], E_[0:16, :N], AF.Ln, [("SG", 0), "ONEV"], [("SG", 0)], bias=ONEV[0:16, 0:1])
            dve(lambda e: e.tensor_scalar(out=LOGF[0:16, :N], in0=E_[0:16, :N], scalar1=-1.0, scalar2=None, op0=ALU.mult),
                [("SG", 0)], [("T1", 0)])
            lfo = (lfs_o if smp else lf_o)[jf]
            S_.dma(AQ, lambda e: e.dma_start(out=lfo[:, oc0:oc0 + N], in_=LOGF[0:16, :N]), ("T1", 0), reads=[("T1", 0)])
            if ksub <= 3:
                return
            if smp:
                for bs in range(NBS):
                    dve(lambda e, bs=bs: e.tensor_tensor_scan(
                        out=Ft[0:16, bs * TT:(bs + 1) * TT], data0=LOGF[0:16, bs * TT:(bs + 1) * TT],
                        data1=ZERB[0:16, 0:TT], initial=0.0, op0=ALU.add, op1=ALU.add), [("T1", 0), "ZERB"], [("T2", 0)])
            else:
                dve(lambda e: e.tensor_tensor_scan(out=Ft[0:16, :N], data0=LOGF[0:16, :N], data1=ZERB[0:16, :N],
                                                   initial=FLAST[0:16, 0:1], op0=ALU.add, op1=ALU.add),
                    [("T1", 0), "ZERB", "FLAST"], [("T2", 0)])
                dve(lambda e: e.tensor_copy(out=FLAST[0:16, 0:1], in_=Ft[0:16, N - 1:N]), [("T2", 0)], ["FLAST"])
            FA, NFA = PTT, OTT
            dve(lambda e: e.tensor_copy(out=FA[0:16, 0:N], in_=Ft[0:16, :N]), [("T2", 0)], ["PTT"])
            dve(lambda e: e.tensor_tensor(out=R1[0:16, :N], in0=Ft[0:16, :N], in1=FA[0:16, 0:N], op=ALU.subtract),
                [("T2", 0), "PTT"], [("SG", 1)])
            dve(lambda e: e.tensor_copy(out=FA[0:16, 512:512 + N], in_=R1[0:16, :N]), [("SG", 1)], ["PTT"])
            dve(lambda e: e.tensor_tensor(out=R2[0:16, :N], in0=R1[0:16, :N], in1=FA[0:16, 512:512 + N], op=ALU.subtract),
                [("SG", 1), "PTT"], [("T1", 1)])
            dve(lambda e: e.tensor_copy(out=FA[0:16, 1024:1024 + N], in_=R2[0:16, :N]), [("T1", 1)], ["PTT"])
            dve(lambda e: e.tensor_scalar(out=NFA[0:16, :], in0=FA[0:16, :], scalar1=-1.0, scalar2=None, op0=ALU.mult),
                ["PTT"], ["OTT"])
            fa3 = v3(FA[0:16, :], 512)[:, :, 0:N]
            nfa3 = v3(NFA[0:16, :], 512)[:, :, 0:N]
            on3 = v3(ONES3[0:16, :], 512)[:, :, 0:N]
            S_.dma(AQ, lambda e: e.dma_start(out=qTs[:, 64:67, c0:c0 + N], in_=fa3), "PTT", reads=["PTT"],
                   writes=[("aug_scr", 0, c0)])
            S_.dma(AQ, lambda e: e.dma_start(out=qTs[:, 67:70, c0:c0 + N], in_=on3), "ONES3", reads=["ONES3"],
                   writes=[("aug_scr", 1, c0)])
            S_.dma(AQ, lambda e: e.dma_start(out=kTs[:, 64:67, c0:c0 + N], in_=on3), "ONES3", reads=["ONES3"],
                   writes=[("aug_scr", 2, c0)])
            S_.dma(AQ, lambda e: e.dma_start(out=kTs[:, 67:70, c0:c0 + N], in_=nfa3), "OTT", reads=["OTT"],
                   writes=[("aug_scr", 3, c0)])

        def scr_cols_keys(cs):
            ks = []
            for c0 in cs:
                ks += [("qk_scr", 0, c0), ("qk_scr", 1, c0)] + [("aug_scr", i, c0) for i in range(4)]
            return ks

        def attention_prompt(jf):
            allc = [i * 512 for i in range(NTL)]
            rk = scr_cols_keys(allc) + [("v_scr", kb) for kb in range(NKB)]
            PTG = [PTT[:, 0:1024], PTT2[:, 0:1024], PTT3[:, 0:1024]]
            pending = [None]
            ob, okey = PS[6], PK[6]
            bcb, bkey = PS[7], PK[7]

            def finish2(h, qb):
                mm(bcb, ONESF[64:65, 0:128], RR[64:65, :], True, True, ["ONESF", "RR"], [bkey])
                ot = OTT[0:64, (qb % 2) * 512:(qb % 2) * 512 + 512]
                dve(lambda e, ot=ot: e.tensor_tensor(out=ot, in0=RR[0:64, :], in1=bcb[0:64, :], op=ALU.mult),
                    ["RR", bkey], [("OTTb", qb % 2)])
                S_.dma(AQ, lambda e, h=h, qb=qb, ot=ot: e.dma_start(out=oTs[h, :, qb * 512:qb * 512 + 512], in_=ot),
                       ("OTTb", qb % 2), reads=[("OTTb", qb % 2)], writes=[("o_scr", qb * 512)])

            for h in range(NH):
                S_.dma(AQ, lambda e, h=h: e.dma_start(out=B1[0:70, 0:S], in_=kTs[h, :, 0:S]), "B1ld", reads=rk,
                       writes=KS("B1", 0, S))
                S_.dma(AQ, lambda e, h=h: e.dma_start(out=B2[0:70, 0:S], in_=qTs[h, :, 0:S]), "B2ld", reads=rk,
                       writes=KS("B2", 0, S))
                S_.dma(AQ, lambda e, h=h: e.dma_start(out=XB[:, 0:NKB * 65], in_=vBs[h].rearrange("p k e -> p (k e)")),
                       "XBld", reads=rk, writes=KS("XB", 0, NKB * 65))
                for qb in range(NTL):
                    nk = 4 * (qb + 1)
                    groups = [list(range(i, min(i + 2, nk))) for i in range(0, nk, 2)]
                    ng = len(groups)

                    def emit_qk(gi):
                        grp = groups[gi]
                        gsel = gi % 3
                        for t, kb in enumerate(grp):
                            sl = PSG[gsel][:, t * 512:t * 512 + 512]
                            mm(sl, B1[0:70, kb * 128:kb * 128 + 128], B2[0:70, qb * 512:qb * 512 + 512], True, True,
                               KS("B1", kb * 128, kb * 128 + 128) + KS("B2", qb * 512, qb * 512 + 512), [PK[2 * gsel + t]])
                            if kb >= 4 * qb:
                                jm = kb - 4 * qb
                                dve(lambda e, sl=sl, jm=jm: e.tensor_tensor(
                                    out=sl, in0=sl, in1=MASK[:, jm * 512:jm * 512 + 512], op=ALU.add),
                                    [PK[2 * gsel + t], "MASK"], [PK[2 * gsel + t]])
                        n = len(grp)
                        act(PTG[gsel][:, 0:n * 512], PSG[gsel][:, 0:n * 512], AF.Exp, [PK[2 * gsel + t] for t in range(n)],
                            [("PTG", gsel)])

                    def emit_pv(gi):
                        grp = groups[gi]
                        gsel = gi % 3
                        for t, kb in enumerate(grp):
                            mm(ob[0:65, :], XB[:, kb * 65:kb * 65 + 65], PTG[gsel][:, t * 512:t * 512 + 512],
                               kb == 0, kb == nk - 1, KS("XB", kb * 65, kb * 65 + 65) + [("PTG", gsel)], [okey])

                    emit_qk(0)
                    if ng > 1:
                        emit_qk(1)
                    if pending[0] is not None:
                        pending[0]()
                        pending[0] = None
                    for gi in range(ng):
                        if gi + 2 < ng:
                            emit_qk(gi + 2)
                        if gi >= 2 and NDUMMY:
                            for _ in range(NDUMMY):
                                mm(bcb, B1[0:70, 0:128], B2[0:70, qb * 512:qb * 512 + 512], True, True,
                                   KS("B1", 0, 128) + KS("B2", qb * 512, qb * 512 + 512), [bkey])
                        emit_pv(gi)
                    dve(lambda e: e.tensor_copy(out=RR[0:65, :], in_=ob[0:65, :]), [okey], ["RR"])
                    dve(lambda e: e.reciprocal(out=RR[64:65, :], in_=RR[64:65, :]), ["RR"], ["RR"])
                    pending[0] = (lambda h=h, qb=qb: finish2(h, qb))
            if pending[0] is not None:
                pending[0]()
                pending[0] = None

        def attention_sample(jf):
            rk = scr_cols_keys([S])
            FC, CL, SS, RA = Z[:, 0:P], Z[:, 1024:1024 + P], Z[:, 2048:2048 + P], Z[:, 3072:3072 + P]
            zk = [("Z", i) for i in range(8)]
            S_.dma(AQ, lambda e: e.dma_start(out=CL, in_=clf[jf]), "Zld", writes=zk)
            dve(lambda e: e.tensor_tensor_scan(out=FC, data0=CL, data1=ZERB[:, 0:P], initial=0.0, op0=ALU.add, op1=ALU.add),
                zk + ["ZERB"], zk)
            dve(lambda e: e.tensor_scalar(out=SS, in0=FC, scalar1=Z[:, P - 1:P], scalar2=-1.0, op0=ALU.subtract, op1=ALU.mult),
                zk, zk)
            b2k = KS("B2", 0, 3 * P)
            dve(lambda e: e.tensor_copy(out=B2[:, 0:P], in_=SS), zk, b2k)
            dve(lambda e: e.tensor_tensor(out=RA, in0=SS, in1=B2[:, 0:P], op=ALU.subtract), zk + b2k, zk)
            dve(lambda e: e.tensor_copy(out=B2[:, P:2 * P], in_=RA), zk, b2k)
            dve(lambda e: e.tensor_tensor(out=CL, in0=RA, in1=B2[:, P:2 * P], op=ALU.subtract), zk + b2k, zk)
            dve(lambda e: e.tensor_copy(out=B2[:, 2 * P:3 * P], in_=CL), zk, b2k)
            S_.dma(AQ, lambda e: e.dma_start(out=ssS, in_=v3(B2[:, 0:3 * P], P)), "B2ld", reads=b2k, writes=["ss_scr"])
            KN0, QS0 = 4096, 6144
            S_.dma(AQ, lambda e: e.dma_start(out=v3(B2[0:70, KN0:KN0 + 2048], 128)[:, :, 0:NS],
                                               in_=kTs.rearrange("h p s -> p h s")[:, :, S:S + NS]), "B2ld",
                   reads=rk, writes=KS("B2", KN0, KN0 + 2048))
            S_.dma(AQ, lambda e: e.dma_start(out=v3(B2[0:70, QS0:QS0 + 2048], 128)[:, :, 0:NS],
                                               in_=qTs.rearrange("h p s -> p h s")[:, :, S:S + NS]), "B2ld",
                   reads=rk, writes=KS("B2", QS0, QS0 + 2048))
            b1k = KS("B1", 0, NH * P)
            dve(lambda e: e.memset(B1[64:70, 0:NH * P], 1.0), [], b1k)
            nkb = P // 128
            VC = WOT
            vck = [("wo", 0), ("wo", 1)]
            for bs in range(NBS):
                S_.dma("pool", lambda e, bs=bs: e.dma_start(out=v3(B1[0:64, 0:NH * P], P),
                                                            in_=ckT[jf, bs].rearrange("h d p -> d h p")), "B1ld", writes=b1k)
                S_.dma(AQ, lambda e, bs=bs: e.dma_start(out=v3(B1[67:70, 0:NH * P], P),
                                                          in_=ssS[bs * NH:(bs + 1) * NH].rearrange("h r p -> r h p")),
                       "B1ld", reads=["ss_scr"], writes=b1k)
                S_.dma("pool", lambda e, bs=bs: e.dma_start(out=v3(VC[:, 0:nkb * 1024], 1024),
                                                            in_=cv[jf, bs].rearrange("(k p) f -> p k f", p=128)),
                       "VCld", writes=vck)
                for h in range(NH):
                    sbk = PS[h % 2]
                    qrhs = B2[0:70, QS0 + h * 128 + bs * TT:QS0 + h * 128 + (bs + 1) * TT]
                    for kb in range(nkb):
                        mm(sbk[:, kb * TT:(kb + 1) * TT], B1[0:70, h * P + kb * 128:h * P + kb * 128 + 128], qrhs, True, True,
                           b1k + KS("B2", QS0, QS0 + 2048), [PK[h % 2]])
                    nc0 = nkb * TT
                    mm(sbk[:, nc0:nc0 + TT], B2[0:70, KN0 + h * 128:KN0 + h * 128 + 128], qrhs, True, True,
                       KS("B2", KN0, KN0 + 4096), [PK[h % 2]])
                    pt = PTT[:, (h % 3) * 512:(h % 3) * 512 + 512]
                    ptk = ("PTG", 0)
                    act(pt[:, 0:nc0], sbk[:, 0:nc0], AF.Exp, [PK[h % 2]], [ptk, ("sbser", h % 2)])
                    w3 = W3[h % 2]
                    dve(lambda e, w3=w3, sbk=sbk, bs=bs: e.tensor_tensor(
                        out=w3[:, 0:TT], in0=sbk[:, nc0:nc0 + TT], in1=MASKS[:, bs * TT:(bs + 1) * TT], op=ALU.add),
                        [PK[h % 2], "MASKS"], [("W3", h % 2), ("sbser", h % 2)])
                    act(pt[:, nc0:nc0 + TT], w3[:, 0:TT], AF.Exp, [("W3", h % 2)], [ptk])
                    ob, rsb = PS[4 + h % 2], PS[6 + h % 2]
                    hp, par = h // 2, h % 2
                    for kb in range(nkb):
                        mm(ob[:, 0:TT], VC[:, kb * 1024 + hp * 128:kb * 1024 + hp * 128 + 128], pt[:, kb * TT:(kb + 1) * TT],
                           kb == 0, False, vck + [ptk], [PK[4 + h % 2]])
                    mm(ob[:, 0:TT], VNS[:, hp * 128:hp * 128 + 128], pt[:, nc0:nc0 + TT], False, True, ["VNS", ptk],
                       [PK[4 + h % 2]])
                    for kb in range(nkb + 1):
                        mm(rsb[:, 0:TT], ONES128[:, :], pt[:, kb * TT:(kb + 1) * TT], kb == 0, kb == nkb, ["ONES128", ptk],
                           [PK[6 + h % 2]])
                    pp0 = par * 64
                    dve(lambda e, rsb=rsb, pp0=pp0: e.reciprocal(out=RRs[pp0:pp0 + 64, :], in_=rsb[pp0:pp0 + 64, 0:TT]),
                        [PK[6 + h % 2]], ["RRs"])
                    dve(lambda e, ob=ob, hp=hp, bs=bs, pp0=pp0: e.tensor_tensor(
                        out=OSALL[pp0:pp0 + 64, hp * 128 + bs * TT:hp * 128 + (bs + 1) * TT], in0=ob[pp0:pp0 + 64, 0:TT],
                        in1=RRs[pp0:pp0 + 64, :], op=ALU.mult), [PK[4 + h % 2], "RRs"], ["OSALL"])
            S_.dma(AQ, lambda e: e.dma_start(out=oTs.rearrange("(hp two) p s -> (two p) hp s", two=2)[:, :, S:S + NS],
                                               in_=v3(OSALL[:, :], 128)[:, :, 0:NS]), "OSALL", reads=["OSALL"],
                   writes=[("o_scr", S)])

        def fox_out(jf, tl):
            N, c0 = tl["N"], tl["c0"]
            S_.dma(AQ, lambda e: e.dma_start(out=v3(X[:, :], 512)[:, :, 0:N],
                                               in_=x1s.rearrange("(c p) s -> p c s", p=128)[:, :, c0:c0 + N]), "Xld",
                   reads=[("x1_scr", c0)], writes=[("X", c) for c in range(8)])
            S_.dma(AQ, lambda e: e.dma_start(out=v3(B1[:, 0:4096], 512)[:, :, 0:N],
                                               in_=oTs.rearrange("(hp two) p s -> (two p) hp s", two=2)[:, :, c0:c0 + N]),
                   "B1ld", reads=[("o_scr", c0)], writes=KS("B1", 0, 4096))
            for s2 in range(4):
                slot = wo_ld(2048, wsc_fo[jf, s2], [("wsc_fo", jf, s2)])
                for cc in range(2):
                    c = 2 * s2 + cc
                    py = PS[4 + c % 2]
                    for hp in range(8):
                        o0 = slot * 5632 + hp * 256 + cc * 128
                        mm(py[:, :N], WOT[:, o0:o0 + 128], B1[:, hp * 512:hp * 512 + N], hp == 0, hp == 7,
                           [("wo", slot), ("B1", hp)], [PK[4 + c % 2]])
                    dve(lambda e, c=c, py=py: e.scalar_tensor_tensor(
                        out=Z[:, c * 512:c * 512 + N], in0=X[:, c * 512:c * 512 + N], scalar=ALPHA, in1=py[:, :N],
                        op0=ALU.mult, op1=ALU.add), [("X", c), PK[4 + c % 2]], [("Z", c)])

        def poolmix(jp, l, tl, first, last):
            N, c0, smp = tl["N"], tl["c0"], tl["smp"]
            nseq, T = (NBS, TT) if smp else (1, N)
            L = 16 + T
            if smp:
                S_.dma(AQ, lambda e: e.dma_start(out=v3(SPOOL[:, :], NBS * 15),
                                                   in_=spool[jp].rearrange("(c p) b t -> p c (b t)", p=128)), "SPOOL",
                       writes=["SPOOL"])
                pov = pools_o[jp].rearrange("(c p) b t -> p c b t", p=128)
                for c in range(8):
                    S_.dma(AQ, lambda e, c=c: e.dma_start(out=pov[:, c, :, :],
                                                            in_=v3(X[:, c * 512:c * 512 + NS], TT)[:, :, 1:16]), "Xst",
                           reads=[("X", c)])
            elif last:
                S_.dma(AQ, lambda e: e.dma_start(out=pool_o[jp].rearrange("(c p) t -> p c t", p=128),
                                                   in_=v3(X[:, :], 512)[:, :, N - 15:N]), "Xst",
                       reads=[("X", c) for c in range(8)])
            slot = win_ld(lambda w: v3(w[:, 0:4096], 512)[:, :, 0:256], v3(wsc_pw[jp], 256), [("wsc_pw", jp)])
            WKv = [v3(W3[i][:, 0:nseq * L], L) for i in range(3)]
            wk = [("W3", i) for i in range(3)]
            for c in range(8):
                w = POOL_W[c]
                xc3 = v3(X[:, c * 512:c * 512 + nseq * T], T)
                if smp:
                    hsrc = v3(SPOOL[:, c * NBS * 15:(c + 1) * NBS * 15], 15)
                    hk = "SPOOL"
                else:
                    hsrc = v3(HIST[:, c * 16:(c + 1) * 16], 16)[:, :, 1:16]
                    hk = "HIST"
                dve(lambda e, hsrc=hsrc: e.tensor_copy(out=WKv[0][:, :, 1:16], in_=hsrc), [hk], [wk[0]])
                dve(lambda e, xc3=xc3: e.tensor_copy(out=WKv[0][:, :, 16:L], in_=xc3), [("X", c)], [wk[0]])
                if not smp:
                    dve(lambda e, c=c: e.tensor_copy(out=HIST[:, c * 16 + 1:c * 16 + 16], in_=X[:, c * 512 + N - 15:c * 512 + N]),
                        [("X", c)], ["HIST"])
                a, b_ = 0, 1
                sh = 1
                lo = 1
                while sh < w:
                    src, dst = WKv[a], WKv[b_]
                    lo2 = lo + sh
                    dve(lambda e, src=src, dst=dst, lo2=lo2, sh=sh: e.tensor_tensor(
                        out=dst[:, :, lo2:L], in0=src[:, :, lo2:L], in1=src[:, :, lo2 - sh:L - sh], op=ALU.add),
                        [wk[a]], [wk[b_]])
                    lo = lo2
                    sh *= 2
                    a, b_ = b_, (2 if b_ == 1 else 1)
                sw = WKv[a]
                plb = v3(B1[:, c * 512:c * 512 + nseq * T], T)
                dve(lambda e, sw=sw, plb=plb, xc3=xc3, w=w: e.scalar_tensor_tensor(
                    out=plb, in0=sw[:, :, 16:L], scalar=1.0 / w, in1=xc3, op0=ALU.mult, op1=ALU.subtract),
                    [wk[a], ("X", c)], [("B1", c)])
                if first and not smp:
                    t1 = T1[c % 2]
                    dve(lambda e, t1=t1, a=a, c=c: e.tensor_tensor(out=t1[:, 0:16], in0=W3[a][:, 16:32],
                                                                   in1=INVC[:, c * 16:(c + 1) * 16], op=ALU.mult),
                        [wk[a], "INVC"], [("T1", c % 2)])
                    dve(lambda e, t1=t1, c=c: e.tensor_tensor(out=B1[:, c * 512:c * 512 + 16], in0=t1[:, 0:16],
                                                              in1=X[:, c * 512:c * 512 + 16], op=ALU.subtract),
                        [("T1", c % 2), ("X", c)], [("B1", c)])
            for g in range(4):
                for dd in range(2):
                    co = 2 * g + dd
                    py = PS[4 + co % 2]
                    for cc in range(2):
                        ci = 2 * g + cc
                        mm(py[:, :N], WIN[slot][:, ci * 512 + dd * 128:ci * 512 + dd * 128 + 128], B1[:, ci * 512:ci * 512 + N],
                           cc == 0, cc == 1, [("win", slot), ("B1", ci)], [PK[4 + co % 2]])
                    t1 = T1[co % 2]
                    S_.op("act", lambda e, t1=t1, co=co: e.mul(out=t1[:, :N], in_=X[:, co * 512:co * 512 + N], mul=ALPHA),
                          reads=[("X", co)], writes=[("T1", co % 2)])
                    dve(lambda e, t1=t1, py=py, co=co: e.scalar_tensor_tensor(
                        out=Z[:, co * 512:co * 512 + N], in0=py[:, :N], scalar=PSC[:, jp * 8 + co:jp * 8 + co + 1],
                        in1=t1[:, :N], op0=ALU.mult, op1=ALU.add), [PK[4 + co % 2], ("T1", co % 2), "PSC"], [("Z", co)])

        def load_x(tl):
            N, c0 = tl["N"], tl["c0"]
            src = (xsT if tl["smp"] else xT).rearrange("(c p) s -> p c s", p=128)
            sc0 = 0 if tl["smp"] else c0
            S_.dma("sp" if (tl["first"] and not tl["smp"]) else AQ, lambda e: e.dma_start(out=v3(X[:, :], 512)[:, :, 0:N], in_=src[:, :, sc0:sc0 + N]), "Xld",
                   writes=[("X", c) for c in range(8)])
            for c in range(8):
                act(XB[:, c * 512:c * 512 + N], X[:, c * 512:c * 512 + N], AF.Copy, [("X", c)], [("XB", c)])

        def store_x1(tl):
            N, c0 = tl["N"], tl["c0"]
            S_.dma(AQ, lambda e: e.dma_start(out=x1s.rearrange("(c p) s -> p c s", p=128)[:, :, c0:c0 + N],
                                               in_=v3(X[:, :], 512)[:, :, 0:N]), "Xst", reads=[("X", c) for c in range(8)],
                   writes=[("x1_scr", c0)])

        def store_y(tl):
            N, c0 = tl["N"], tl["c0"]
            dst = (ysT if tl["smp"] else yT).rearrange("(c p) s -> p c s", p=128)
            oc0 = 0 if tl["smp"] else c0
            S_.dma(AQ, lambda e: e.dma_start(out=dst[:, :, oc0:oc0 + N], in_=v3(X[:, :], 512)[:, :, 0:N]), "Xst",
                   reads=[("X", c) for c in range(8)])

        tiles = [dict(N=512, c0=i * 512, smp=False, first=(i == 0), last=(i == NTL - 1)) for i in range(NTL)]
        tiles.append(dict(N=NS, c0=S, smp=True, first=True, last=True))

        def layer_tail(l, tl):
            ffn(l, 1, tl["N"])
            ple(l, tl)

        def run_layers_from(l, tl, started):
            while l < depth:
                if not started:
                    ffn(l, 0, tl["N"])
                    if l % 2 == 0:
                        store_x1(tl)
                        foxproj(l // 2, tl)
                        return
                    poolmix(l // 2, l, tl, tl["first"], tl["last"])
                    layer_norm(l * 4 + 1, tl["N"])
                layer_tail(l, tl)
                started = False
                l += 1
            store_y(tl)

        def reset_carry():
            dve(lambda e: e.memset(FLAST[:, :], 0.0), [], ["FLAST"])
            dve(lambda e: e.memset(HIST[:, :], 0.0), [], ["HIST"])

        import os
        dbg = int(os.environ.get("KDBG", "0"))
        reset_carry()
        load_x(tiles[0])
        for l_ in range(depth):
            cast_ffn(l_, 0)
            if l_ % 2 == 0:
                cast_fox_in(l_ // 2); cast_fox_o(l_ // 2)
            else:
                cast_pool(l_ // 2)
            cast_ffn(l_, 1)
            cast_ple(l_)
        if dbg:
            tl = tiles[0] if dbg < 20 else tiles[-1]
            d2 = dbg % 20
            load_x(tl)
            if d2 >= 2 and not os.environ.get("NOFFN"):
                ffn(0, 0, tl["N"])
            if d2 >= 3:
                store_x1(tl)
                foxproj(0, tl)
            if d2 >= 4:
                for t2 in tiles[1:]:
                    load_x(t2); store_x1(t2); foxproj(0, t2)
            if d2 >= 5:
                attention_prompt(0)
            if d2 >= 6:
                attention_sample(0)
            if d2 >= 7:
                fox_out(0, tl); layer_norm(1, tl["N"])
            if d2 >= 8:
                ple(0, tl)
            if d2 >= 9:
                poolmix(0, 1, tl, True, True); layer_norm(5, tl["N"])
            store_y(tl)
            depth = 0
        else:
          for tl in tiles:
            if not (tl["first"] and not tl["smp"]):
                load_x(tl)
            run_layers_from(0, tl, False)
        l = 0
        while l < depth:
            jf = l // 2
            attention_prompt(jf)
            attention_sample(jf)
            reset_carry()
            for tl in tiles:
                fox_out(jf, tl)
                layer_norm(l * 4 + 1, tl["N"])
                run_layers_from(l, tl, True)
            l += 2
        S_.emit(nc)
    return nc


NDUMMY = 0
REAL_CORES = [0, 1, 4, 5]


def _consts(NBS):
    p = np.arange(128)[:, None]
    f = np.arange(512)[None, :]
    maskp = np.concatenate([np.where(f >= j * 128 + p, 0.0, NEG) for j in range(4)], axis=1).astype(np.float32)
    q = np.arange(TT)[None, :]
    masks = np.concatenate([np.where((p // TT == bs) & (p % TT <= q), 0.0, NEG) for bs in range(NBS)], axis=1).astype(np.float32)
    invc = np.zeros((128, 128), np.float32)
    for c in range(8):
        w = POOL_W[c]
        invc[:, c * 16:(c + 1) * 16] = 1.0 / np.minimum(np.arange(16) + 1, w)
    return maskp, masks, invc


def _vec_layout(a):
    lead = int(np.prod(a.shape[:-1]))
    return np.ascontiguousarray(a.reshape(lead, 8, 128).transpose(2, 0, 1).reshape(128, lead * 8)).astype(np.float32)


def make_in_maps(inp, n_cores, NBS):
    f = lambda a: np.ascontiguousarray(np.asarray(a, dtype=np.float32))
    B = inp["x_prompt"].shape[0]
    maskp, masks, invc = _consts(NBS)
    shared = {
        "ffn_w_in": f(inp["ffn_w_in"]), "ffn_w_out": f(inp["ffn_w_out"]), "fox_w_in": f(inp["fox_w_in"]),
        "fox_w_o": f(inp["fox_w_o"]), "pool_w": f(inp["pool_w"]), "ple_w_proj": f(inp["ple_w_proj"]),
        "ple_w_gate": f(inp["ple_w_gate"]),
        "lng": _vec_layout(np.asarray(inp["ln_g"])), "lnb": _vec_layout(np.asarray(inp["ln_b"])),
        "bgate": _vec_layout(np.asarray(inp["ple_b_gate"])), "pscale": _vec_layout(np.asarray(inp["pool_scale"])),
        "bf": f(np.asarray(inp["fox_b_f"]).T), "maskp": maskp, "masks": masks, "invc": invc,
    }
    maps = []
    zero_map = None
    for core in range(n_cores):
        if core not in REAL_CORES:
            if zero_map is None:
                zero_map = {k: np.zeros_like(v) for k, v in maps[0].items()}
            maps.append(zero_map)
            continue
        b = REAL_CORES.index(core)
        sl = slice(b * NBS, (b + 1) * NBS)
        m = dict(shared)
        m["xT"] = f(np.asarray(inp["x_prompt"])[b].T)
        m["pT"] = f(np.asarray(inp["p_prompt"])[:, b].transpose(0, 2, 1))
        xs = np.asarray(inp["x_sample"])[sl]
        m["xsT"] = f(xs.reshape(NBS * TT, D).T)
        ps = np.asarray(inp["p_sample"])[:, sl]
        m["psT"] = f(ps.reshape(ps.shape[0], NBS * TT, PLE).transpose(0, 2, 1))
        m["ckT"] = f(np.asarray(inp["cache_fox_k"])[:, sl].transpose(0, 1, 3, 4, 2))
        ck = np.asarray(inp["cache_fox_v"])[:, sl]
        m["cv"] = f(ck.reshape(ck.shape[0], NBS, ck.shape[2], D))
        cl = np.asarray(inp["cache_fox_logf"])[:, sl].transpose(0, 1, 3, 2)
        m["clf"] = f(cl.reshape(cl.shape[0], NBS * NH, cl.shape[3]))
        m["spool"] = f(np.asarray(inp["state_pool"])[:, sl].transpose(0, 3, 1, 2))
        maps.append(m)
    return maps


def assemble(results, B, S, NBS):
    results = [results[c] for c in REAL_CORES[:B]]
    NF = results[0]["kT_o"].shape[0]
    NP = results[0]["pool_o"].shape[0]
    DB = B * NBS
    yp = np.stack([results[b]["yT"].T for b in range(B)])
    ys = np.concatenate([results[b]["ysT"].T.reshape(NBS, TT, D) for b in range(B)])
    kp = np.stack([results[b]["kT_o"].transpose(0, 3, 1, 2) for b in range(B)], axis=1)
    vp = np.stack([results[b]["v_o"].reshape(NF, S, NH, DH) for b in range(B)], axis=1)
    fp = np.stack([results[b]["lf_o"].transpose(0, 2, 1) for b in range(B)], axis=1)
    pp = np.stack([results[b]["pool_o"].transpose(0, 2, 1) for b in range(B)], axis=1)
    ks = np.concatenate([results[b]["kTs_o"].transpose(0, 3, 1, 2).reshape(NF, NBS, TT, NH, DH) for b in range(B)], axis=1)
    vs = np.concatenate([results[b]["vs_o"].reshape(NF, NBS, TT, NH, DH) for b in range(B)], axis=1)
    fs = np.concatenate([results[b]["lfs_o"].transpose(0, 2, 1).reshape(NF, NBS, TT, NH) for b in range(B)], axis=1)
    psm = np.concatenate([results[b]["pools_o"].transpose(0, 2, 3, 1) for b in range(B)], axis=1)
    outs = (yp, ys, kp, vp, fp, pp, ks, vs, fs, psm)
    return tuple(np.ascontiguousarray(o, dtype=np.float32) for o in outs)


def kernel(**inputs):
    B, S, _ = inputs["x_prompt"].shape
    DB = inputs["x_sample"].shape[0]
    P = inputs["cache_fox_k"].shape[2]
    NBS = DB // B
    n_cores = 8
    nc = build_program(S, NBS, P)
    in_maps = make_in_maps(inputs, n_cores, NBS)
    res = run_bass_kernel_spmd(nc, in_maps, core_ids=list(range(n_cores)))
    return assemble(res.results, B, S, NBS)
```
